# Optimizing a Trainium2 kernel written in Bass

```python
import math
import jax, jax.numpy as jnp
from jax import lax
import numpy as np


D_MODEL = 1024
BATCH = 8
SEQ = 4096
DEPTH = 4

GRID_W = 64
CTX_LEN = 256
D_MIX = D_MODEL
BRANCH_W = D_MIX // 2
A_HEADS = 8
A_QK = BRANCH_W // (2 * A_HEADS)
A_V = 2 * A_QK
CONV_W = 3
C_HEADS = 8
C_HEAD_DIM = BRANCH_W // C_HEADS
NA_KH = 8
NA_KW = 16
D_GROUPS = 8
D_GROUP_W = BRANCH_W // D_GROUPS
Q_BLOCK = 128
ROPE_BASE = 10000.0
EPS = 1e-6
N_EVEN = (DEPTH + 1) // 2
N_ODD = DEPTH // 2
EVEN_IN = 8 * BRANCH_W
ODD_IN = 6 * BRANCH_W

kernel_name = 'hybrid_diffusion_backbone'


def rms_norm(x, g):
    xf = x.astype(jnp.float32)
    y = xf * lax.rsqrt(jnp.mean(xf * xf, axis=-1, keepdims=True) + EPS)
    return (y * g.astype(jnp.float32)).astype(x.dtype)


def lambda_init(layer):
    return 0.8 - 0.6 * math.exp(-0.3 * layer)


def axial_rope(x, row, col):
    half = x.shape[-1] // 2
    nf = half // 2
    inv_freq = ROPE_BASE ** (-jnp.arange(nf, dtype=jnp.float32) / nf)

    def rotate(u, pos):
        ang = pos.astype(jnp.float32)[:, None] * inv_freq
        cos = jnp.cos(ang)[None, :, None, None, :]
        sin = jnp.sin(ang)[None, :, None, None, :]
        u1, u2 = u[..., :nf], u[..., nf:]
        return jnp.concatenate([u1 * cos - u2 * sin, u2 * cos + u1 * sin], axis=-1)

    xf = x.astype(jnp.float32)
    y = jnp.concatenate([rotate(xf[..., :half], row), rotate(xf[..., half:], col)], axis=-1)
    return y.astype(x.dtype)


def diff_attend(q, k, v, lam):
    s = jnp.einsum('bqhmd,bkhmd->bhmqk', q, k).astype(jnp.float32) * (A_QK ** -0.5)
    p = jax.nn.softmax(s, axis=-1)
    a = p[:, :, 0] - lam * p[:, :, 1]
    return jnp.einsum('bhqk,bkhd->bqhd', a, v.astype(jnp.float32))


def diff_attention_post(o, subln, lam_init):
    B, N = o.shape[:2]
    return (rms_norm(o, subln) * (1.0 - lam_init)).reshape(B, N, BRANCH_W)


def short_conv(b, c, u, w):
    z = c * u
    y = lax.conv_general_dilated(z, w.reshape(CONV_W, 1, BRANCH_W).astype(z.dtype),
                                 window_strides=(1,), padding='SAME',
                                 dimension_numbers=('NWC', 'WIO', 'NWC'),
                                 feature_group_count=BRANCH_W)
    return b * y


def fourier_mix(u):
    B, N, _ = u.shape
    ug = u.astype(jnp.float32).reshape(B, N, D_GROUPS, D_GROUP_W)
    y = jnp.fft.fft2(ug, axes=(1, 3), norm='ortho').real
    return y.reshape(B, N, BRANCH_W).astype(u.dtype)


def dense_attend(q, k, v):
    s = jnp.einsum('bqhd,bkhd->bhqk', q, k).astype(jnp.float32) * (C_HEAD_DIM ** -0.5)
    p = jax.nn.softmax(s, axis=-1)
    return jnp.einsum('bhqk,bkhd->bqhd', p, v.astype(jnp.float32))


def neighbourhood_attention(q, k, v, k_c, v_c, rpb):
    B, S, H, d = q.shape
    rows = S // GRID_W
    kh = min(NA_KH, rows)
    kw = NA_KW
    r = jnp.arange(rows)
    cq = jnp.arange(GRID_W)
    rs = jnp.clip(r - kh // 2, 0, rows - kh)
    cs = jnp.clip(cq - kw // 2, 0, GRID_W - kw)
    col_idx = cs[:, None] + jnp.arange(kw)[None, :]
    row_off = rs[:, None] + jnp.arange(kh)[None, :] - r[:, None] + (NA_KH - 1)
    col_off = col_idx - cq[:, None] + (NA_KW - 1)
    scale = d ** -0.5
    qg = jnp.moveaxis(q.reshape(B, rows, GRID_W, H, d), 1, 0)
    kg = k.reshape(B, rows, GRID_W, H, d)
    vg = v.reshape(B, rows, GRID_W, H, d)
    s_c_all = None

    def one_row(args):
        q_r, r0, roff = args
        kb = lax.dynamic_slice_in_dim(kg, r0, kh, axis=1)
        vb = lax.dynamic_slice_in_dim(vg, r0, kh, axis=1)
        kwin = kb[:, :, col_idx]
        vwin = vb[:, :, col_idx]
        bias = rpb[:, roff][:, :, col_off]
        s_w = jnp.einsum('bqhd,biqjhd->bhqij', q_r, kwin).astype(jnp.float32) * scale
        s_w = s_w + jnp.transpose(bias, (0, 2, 1, 3))[None].astype(jnp.float32)
        s_w = s_w.reshape(B, H, GRID_W, kh * kw)
        s_c = jnp.einsum('bqhd,bkhd->bhqk', q_r, k_c).astype(jnp.float32) * scale
        p = jax.nn.softmax(jnp.concatenate([s_w, s_c], axis=-1), axis=-1)
        p_w = p[..., :kh * kw].reshape(B, H, GRID_W, kh, kw)
        p_c = p[..., kh * kw:]
        return (jnp.einsum('bhqij,biqjhd->bqhd', p_w, vwin.astype(jnp.float32))
                + jnp.einsum('bhqk,bkhd->bqhd', p_c, v_c.astype(jnp.float32)))

    o = lax.map(one_row, (qg, rs, row_off))
    return jnp.moveaxis(o, 0, 1).reshape(B, S, H * d)


def even_mixer(px, pc, lam_p, subln, conv_w, lam_init, row, col, need_ctx):
    B, S, _ = px.shape
    L = pc.shape[1]
    qa, ka, va, ga, bb, cb, ub, gb = jnp.split(px, 8, axis=-1)
    qa = axial_rope(qa.reshape(B, S, A_HEADS, 2, A_QK), row, col)
    ka = axial_rope(ka.reshape(B, S, A_HEADS, 2, A_QK), row, col)
    va = va.reshape(B, S, A_HEADS, A_V)
    if need_ctx:
        qc, kc, vc, gc, bbc, cbc, ubc, gbc = jnp.split(pc, 8, axis=-1)
    else:
        kc, vc = jnp.split(pc, 2, axis=-1)
    kc = kc.reshape(B, L, A_HEADS, 2, A_QK)
    vc = vc.reshape(B, L, A_HEADS, A_V)
    lp = lam_p.astype(jnp.float32)
    lam = jnp.exp(jnp.sum(lp[0] * lp[1])) - jnp.exp(jnp.sum(lp[2] * lp[3])) + lam_init
    k_all = jnp.concatenate([kc, ka], axis=1)
    v_all = jnp.concatenate([vc, va], axis=1)
    nb = S // Q_BLOCK
    q_blocks = jnp.moveaxis(qa.reshape(B, nb, Q_BLOCK, A_HEADS, 2, A_QK), 1, 0)
    o = lax.map(lambda qb: diff_attend(qb, k_all, v_all, lam), q_blocks)
    o = jnp.moveaxis(o, 0, 1).reshape(B, S, A_HEADS, A_V)
    out_a = diff_attention_post(o, subln, lam_init).astype(px.dtype) * jax.nn.silu(ga)
    out_b = short_conv(bb, cb, ub, conv_w) * jax.nn.silu(gb)
    out_x = jnp.concatenate([out_a, out_b], axis=-1)
    if not need_ctx:
        return out_x, None
    oc = diff_attend(qc.reshape(B, L, A_HEADS, 2, A_QK), kc, vc, lam)
    out_ac = diff_attention_post(oc, subln, lam_init).astype(pc.dtype) * jax.nn.silu(gc)
    out_bc = short_conv(bbc, cbc, ubc, conv_w) * jax.nn.silu(gbc)
    return out_x, jnp.concatenate([out_ac, out_bc], axis=-1)


def odd_mixer(px, pc, rpb, need_ctx):
    B, S, _ = px.shape
    L = pc.shape[1]

    def heads(u):
        return u.reshape(u.shape[0], u.shape[1], C_HEADS, C_HEAD_DIM)

    qn, kn, vn, gn, ud, gd = jnp.split(px, 6, axis=-1)
    if need_ctx:
        qc, kc, vc, gc, udc, gdc = jnp.split(pc, 6, axis=-1)
    else:
        kc, vc = jnp.split(pc, 2, axis=-1)
    kc, vc = heads(kc), heads(vc)
    o_n = neighbourhood_attention(heads(qn), heads(kn), heads(vn), kc, vc, rpb)
    out_x = jnp.concatenate([o_n.astype(px.dtype) * jax.nn.silu(gn),
                             fourier_mix(ud) * jax.nn.silu(gd)], axis=-1)
    if not need_ctx:
        return out_x, None
    o_nc = dense_attend(heads(qc), kc, vc).reshape(B, L, BRANCH_W)
    out_c = jnp.concatenate([o_nc.astype(pc.dtype) * jax.nn.silu(gc),
                             fourier_mix(udc) * jax.nn.silu(gdc)], axis=-1)
    return out_x, out_c


def setup_inputs(seed: int = 0) -> dict:
    key = jax.random.key(seed)
    ks = jax.random.split(key, 15)
    f = jnp.float32
    nrm = jax.random.normal
    return {
        'x': nrm(ks[0], (BATCH, SEQ, D_MODEL), f),
        'c': nrm(ks[1], (BATCH, D_MODEL), f),
        'ctx': nrm(ks[2], (BATCH, CTX_LEN, D_MODEL), f),
        'c_ctx': nrm(ks[3], (D_MODEL,), f),
        'w_mod': nrm(ks[4], (DEPTH, D_MODEL, 3 * D_MODEL), f) * D_MODEL ** -0.5,
        'b_mod': 0.02 * nrm(ks[5], (DEPTH, 3 * D_MODEL), f),
        'norm_pre': 1.0 + 0.1 * nrm(ks[6], (DEPTH, D_MODEL), f),
        'norm_post': 1.0 + 0.1 * nrm(ks[7], (DEPTH, D_MODEL), f),
        'w_in_even': nrm(ks[8], (N_EVEN, D_MODEL, EVEN_IN), f) * D_MODEL ** -0.5,
        'lam_a': 0.1 * nrm(ks[9], (N_EVEN, 4, A_QK), f),
        'subln_a': 1.0 + 0.1 * nrm(ks[10], (N_EVEN, A_V), f),
        'conv_b': nrm(ks[11], (N_EVEN, CONV_W, BRANCH_W), f) * CONV_W ** -0.5,
        'w_in_odd': nrm(ks[12], (N_ODD, D_MODEL, ODD_IN), f) * D_MODEL ** -0.5,
        'rpb_c': 0.1 * nrm(ks[13], (N_ODD, C_HEADS, 2 * NA_KH - 1, 2 * NA_KW - 1), f),
        'w_out': nrm(ks[14], (DEPTH, D_MIX, D_MODEL), f) * D_MIX ** -0.5,
    }


def reference(x, c, ctx, c_ctx, w_mod, b_mod, norm_pre, norm_post, w_in_even, lam_a,
              subln_a, conv_b, w_in_odd, rpb_c, w_out):
    S = x.shape[1]
    t = jnp.arange(S)
    row = t // GRID_W
    col = t % GRID_W
    silu_c = jax.nn.silu(c)
    silu_cc = jax.nn.silu(c_ctx)
    for l in range(DEPTH):
        need_ctx = l < DEPTH - 1
        j = l // 2
        mod_x = (silu_c @ w_mod[l] + b_mod[l])[:, None, :]
        mod_c = (silu_cc @ w_mod[l] + b_mod[l])[None, None, :]
        sh_x, sc_x, g_x = jnp.split(mod_x, 3, axis=-1)
        sh_c, sc_c, g_c = jnp.split(mod_c, 3, axis=-1)
        hx = rms_norm(x, norm_pre[l]) * (1.0 + sc_x) + sh_x
        hc = rms_norm(ctx, norm_pre[l]) * (1.0 + sc_c) + sh_c
        w_in = w_in_even[j] if l % 2 == 0 else w_in_odd[j]
        px = hx @ w_in
        pc = hc @ (w_in if need_ctx else w_in[:, BRANCH_W:3 * BRANCH_W])
        if l % 2 == 0:
            mx, mc = even_mixer(px, pc, lam_a[j], subln_a[j], conv_b[j], lambda_init(l),
                                row, col, need_ctx)
        else:
            mx, mc = odd_mixer(px, pc, rpb_c[j], need_ctx)
        x = x + g_x * rms_norm(mx @ w_out[l], norm_post[l])
        if need_ctx:
            ctx = ctx + g_c * rms_norm(mc @ w_out[l], norm_post[l])
    return x
```

```python
import math
from contextlib import ExitStack

import numpy as np
import ml_dtypes
import concourse.bass as bass
import concourse.mybir as mybir
from concourse.bass_utils import run_bass_kernel_spmd

F32 = mybir.dt.float32
BF16 = mybir.dt.bfloat16
AF = mybir.ActivationFunctionType
ALU = mybir.AluOpType
AX = mybir.AxisListType

D = 1024
S = 4096
L = 256
T = S + L
NT = T // 128
DEPTH = 4
EPS = 1e-6
GW = 64
ZC = T + 3
NEG = -30000.0

SAME_ENGINE_SYNC = True
KSTOP = ""
NDS = 40
NSW = 8


_UN = [0]


def un(name):
    _UN[0] += 1
    return "%s_%d" % (name, _UN[0])


class Ins:
    __slots__ = ("eng", "fn", "waits", "flag", "cnt", "idx", "dma")

    def __init__(self, eng, fn):
        self.eng = eng
        self.fn = fn
        self.waits = []
        self.flag = False
        self.cnt = None
        self.idx = None
        self.dma = None


class Sched:
    ENGS = ("pe", "act", "dve", "pool", "sp")

    def __init__(self, nc, stack):
        self.nc = nc
        self.sem = {e: stack.enter_context(nc.semaphore("s_" + e)) for e in self.ENGS}
        self.dsem = [stack.enter_context(nc.semaphore("d%d" % i)) for i in range(NDS + NSW)]
        self.dcnt = [0] * (NDS + NSW)
        self.dnext = 0
        self.dnext_sw = 0
        self.ins = {e: [] for e in self.ENGS}
        self.total = {e: 0 for e in self.ENGS}
        self.nidx = {e: 0 for e in self.ENGS}
        self.last_w = {}
        self.readers = {}
        self.seen = {e: {} for e in self.ENGS}
        self.rings = {}

    def ring(self, name, n):
        i = self.rings.get(name, 0)
        self.rings[name] = i + 1
        return i % n

    def _add_wait(self, rec, ev):
        eng = rec.eng
        if ev[0] == "eng":
            t = ev[1]
            if t.eng == eng and (eng == "pe" or eng == "sp" or not SAME_ENGINE_SYNC):
                return
            if self.seen[eng].get(t.eng, -1) >= t.idx:
                return
            self.seen[eng][t.eng] = t.idx
            t.flag = True
            rec.waits.append(("eng", t))
        else:
            _, j, val = ev
            key = ("d", j)
            if self.seen[eng].get(key, 0) >= val:
                return
            self.seen[eng][key] = val
            rec.waits.append(("dma", j, val))

    def _deps(self, rec, reads, writes):
        for k in reads:
            ev = self.last_w.get(k)
            if ev is not None:
                self._add_wait(rec, ev)
        for k in writes:
            ev = self.last_w.get(k)
            if ev is not None:
                self._add_wait(rec, ev)
            rd = self.readers.get(k)
            if rd:
                for ev2 in rd["eng"].values():
                    self._add_wait(rec, ev2)
                for ev2 in rd["dma"]:
                    self._add_wait(rec, ev2)

    def _record(self, ev, reads, writes):
        for k in reads:
            rd = self.readers.setdefault(k, {"eng": {}, "dma": []})
            if ev[0] == "eng":
                rd["eng"][ev[1].eng] = ev
            else:
                rd["dma"].append(ev)
        for k in writes:
            self.last_w[k] = ev
            self.readers[k] = {"eng": {}, "dma": []}

    def op(self, eng, fn, reads=(), writes=()):
        rec = Ins(eng, fn)
        rec.idx = self.nidx[eng]
        self.nidx[eng] += 1
        self._deps(rec, reads, writes)
        self.ins[eng].append(rec)
        self._record(("eng", rec), reads, writes)
        return rec

    def dma(self, out, in_, reads=(), writes=(), q="sp", slow=False):
        kw = {"allow_slow_non_contiguous": True} if slow else {}
        rec = Ins(q, lambda e: e.dma_start(out=out, in_=in_, **kw))
        rec.idx = self.nidx[q]
        self.nidx[q] += 1
        if q == "pool":
            j = NDS + self.dnext_sw % NSW
            self.dnext_sw += 1
        else:
            j = self.dnext % NDS
            self.dnext += 1
        if self.dcnt[j] > 0:
            self._add_wait(rec, ("dma", j, self.dcnt[j]))
        self._deps(rec, reads, writes)
        self.dcnt[j] += 16
        rec.dma = j
        self.ins[q].append(rec)
        self._record(("dma", j, self.dcnt[j]), reads, writes)
        return rec

    def mm(self, out, lhsT, rhs, start, stop, reads, writes, tp=None):
        kw = {}
        if tp is not None:
            kw["tile_position"] = tp
        return self.op("pe", lambda e: e.matmul(out, lhsT=lhsT, rhs=rhs, start=start, stop=stop,
                                                skip_group_check=True, **kw), reads, writes)

    def tr(self, out, in_, ident, reads, writes):
        return self.op("pe", lambda e: e.transpose(out, in_, ident), reads, writes)

    def act(self, out, in_, func, reads, writes, scale=None, bias=None, accum_out=None):
        kw = {}
        if scale is not None:
            kw["scale"] = scale
        if bias is not None:
            kw["bias"] = bias
        if accum_out is not None:
            kw["accum_out"] = accum_out
        return self.op("act", lambda e: e.activation(out=out, in_=in_, func=func, **kw), reads, writes)

    def tt(self, eng, out, in0, in1, op, reads, writes):
        return self.op(eng, lambda e: e.tensor_tensor(out=out, in0=in0, in1=in1, op=op), reads, writes)

    def ts(self, eng, out, in0, s1, s2, op0, op1, reads, writes):
        if op1 is None:
            return self.op(eng, lambda e: e.tensor_scalar(out=out, in0=in0, scalar1=s1, scalar2=None, op0=op0),
                           reads, writes)
        return self.op(eng, lambda e: e.tensor_scalar(out=out, in0=in0, scalar1=s1, scalar2=s2, op0=op0, op1=op1),
                       reads, writes)

    def stt(self, out, in0, scalar, in1, op0, op1, reads, writes):
        return self.op("dve", lambda e: e.scalar_tensor_tensor(out=out, in0=in0, scalar=scalar, in1=in1,
                                                               op0=op0, op1=op1), reads, writes)

    def cp(self, eng, out, in_, reads, writes):
        if eng == "act":
            return self.act(out, in_, AF.Copy, reads, writes)
        return self.op(eng, lambda e: e.tensor_copy(out=out, in_=in_), reads, writes)

    def red(self, out, in_, reads, writes, op=ALU.add):
        return self.op("dve", lambda e: e.tensor_reduce(out=out, in_=in_, axis=AX.X, op=op), reads, writes)

    def recip(self, out, in_, reads, writes):
        return self.op("dve", lambda e: e.reciprocal(out=out, in_=in_), reads, writes)

    def rsqrt(self, ap, key):
        self.act(ap, ap, AF.Ln, [key], [key])
        self.act(ap, ap, AF.Exp, [key], [key], scale=-0.5)

    def memset(self, eng, ap, val, writes):
        return self.op(eng, lambda e: e.memset(ap, val), (), writes)

    def flush(self):
        nc = self.nc
        for e in self.ENGS:
            for rec in reversed(self.ins[e]):
                if rec.dma is None:
                    rec.flag = True
                    break
        for e in self.ENGS:
            for rec in self.ins[e]:
                if rec.flag and rec.dma is None:
                    self.total[e] += 1
                    rec.cnt = self.total[e]
        lists = {e: self.ins[e] for e in self.ENGS}
        total = dict(self.total)
        dcnt = list(self.dcnt)
        sem = self.sem
        dsem = self.dsem

        def body(eng_name):
            def run(e):
                for rec in lists[eng_name]:
                    for w in rec.waits:
                        if w[0] == "eng":
                            e.wait_ge(sem[w[1].eng], w[1].cnt)
                        else:
                            e.wait_ge(dsem[w[1]], w[2])
                    ins = rec.fn(e)
                    if rec.dma is not None:
                        ins.then_inc(dsem[rec.dma], 16)
                    elif rec.flag:
                        ins.then_inc(sem[eng_name], 1)
                for o in self.ENGS:
                    if o != eng_name and o != "sp" and total[o] > 0:
                        e.wait_ge(sem[o], total[o])
                for j in range(NDS + NSW):
                    if dcnt[j] > 0:
                        e.wait_ge(dsem[j], dcnt[j])
            return run

        with nc.Block() as block:
            block.tensor(body("pe"))
            block.scalar(body("act"))
            block.vector(body("dve"))
            block.gpsimd(body("pool"))
            block.sync(body("sp"))
        self.ins = {e: [] for e in self.ENGS}
        self.last_w = {}
        self.readers = {}
        self.seen = {e: {} for e in self.ENGS}


def lambda_init(layer):
    return 0.8 - 0.6 * math.exp(-0.3 * layer)


def chunks_list():
    res = [(0, L)]
    for c in range(S // 512):
        res.append((L + 512 * c, 512))
    return res


def build(nl=DEPTH, dbg=False):
    nc = bass.Bass("TRN2", target_bir_lowering=False)
    dt = nc.dram_tensor

    def din(name, shape, dtype=F32):
        return dt(name, list(shape), dtype, kind="ExternalInput").ap()

    x_in = din("x_in", [T, D])
    cvec = din("cvec", [128, 16])
    w_mod = din("w_mod", [DEPTH, D, 3 * D])
    b_mod = din("b_mod", [DEPTH, 3 * D])
    norm_pre = din("norm_pre", [DEPTH, D])
    norm_post = din("norm_post", [DEPTH, D])
    w_in_e = din("w_in_e", [2, D, 5120])
    w_in_o = din("w_in_o", [2, D, 3072])
    w_out = din("w_out", [DEPTH, D, D])
    lam_a = din("lam_a", [2, 128])
    subln_a = din("subln_a", [2, 64])
    conv_b = din("conv_b", [2, 128, 12])
    rpbT = din("rpbT", [2, 8, 64, 960])
    ropec = din("ropec", [128, T])
    ropes = din("ropes", [128, T])
    identf = din("identf", [128, 128])
    dftc = din("dftc", [16, 128, 32 * 256], BF16)
    dfts = din("dfts", [16, 128, 32 * 256], BF16)
    dftc_l = din("dftc_l", [128, 2 * 256], BF16)
    dfts_l = din("dfts_l", [128, 2 * 256], BF16)
    cs64 = din("cs64", [128, 256], BF16)
    out = dt("out", [S, D], F32, kind="ExternalOutput").ap()
    xs = dt("xs", [T, D], F32, kind="ExternalOutput" if dbg else "Internal").ap()
    qT_d = dt("qT_d", [512, T], BF16).ap()
    kT_d = dt("kT_d", [512, T], BF16).ap()
    vx_d = dt("vx_d", [T, 520], BF16).ap()
    sg_d = dt("sg_d", [T, 512], F32).ap()
    zT_d = dt("zT_d", [512, ZC], F32).ap()
    wT_d = dt("wT_d", [512, ZC], F32).ap()
    z1_d = dt("z1_d", [T, 512], BF16).ap()
    z2_d = dt("z2_d", [T, 512], BF16).ap()
    yg_d = dt("yg_d", [T, 512], F32).ap()
    sg2_d = dt("sg2_d", [T, 512], F32).ap()

    stack = ExitStack()
    with stack:
        sc = Sched(nc, stack)
        sb = lambda name, shape, dtype=F32: stack.enter_context(nc.sbuf_tensor(un(name), list(shape), dtype))

        ident = sb("ident", [128, 128])
        identb = sb("identb", [128, 128], BF16)
        modA = [sb("modA%d" % i, [128, D]) for i in range(2)]
        modB = [sb("modB%d" % i, [128, D]) for i in range(2)]
        modG = [sb("modG%d" % i, [128, D]) for i in range(2)]
        scT = sb("scT", [128, 2, 8])
        scB = sb("scB", [128, 2, 8, 128])
        ps_all = stack.enter_context(nc.psum_tensor("ps_all", [128, 8, 512], F32))

        sc.dma(ident[:], identf[:, :], (), ["ident"])
        sc.cp("dve", identb[:], ident[:], ["ident"], ["identb"])
        sc.dma(scT[:].rearrange("p a k -> p (a k)"), cvec[:, :], (), ["scT"])
        sc.act(scT[:], scT[:], AF.Silu, ["scT"], ["scT"])
        sc.cp("dve", scB[:], scT[:].unsqueeze(3).to_broadcast([128, 2, 8, 128]), ["scT"], ["scB"])
        sc.flush()

        for l in range(nl):
            even = (l % 2 == 0)
            j = l // 2
            last = (l == DEPTH - 1)
            src_x = x_in if l == 0 else xs
            with ExitStack() as ph:
                psb = lambda name, shape, dtype=F32: ph.enter_context(nc.sbuf_tensor(un(name), list(shape), dtype))
                wst = [psb("wst%d" % i, [128, 8, 512]) for i in range(3)]
                bmb = psb("bmb", [128, 3 * D])
                npre = psb("npre", [128, D])
                npost = psb("npost", [128, D])
                sc.dma(bmb[:], b_mod[l, :].partition_broadcast(128), (), ["bmb"])
                sc.dma(npre[:], norm_pre[l, :].partition_broadcast(128), (), ["npre"])
                sc.dma(npost[:], norm_post[l, :].partition_broadcast(128), (), ["npost"])
                for cc in range(6):
                    wb = sc.ring("wst", 3)
                    sc.dma(wst[wb][:], w_mod[l, :, cc * 512:(cc + 1) * 512].rearrange("(k p) c -> p k c", p=128),
                           (), [("wst", wb)])
                    for who in range(2):
                        pb = sc.ring("psM", 4)
                        pt = ps_all[:, pb, :]
                        for kc in range(8):
                            sc.mm(pt, scB[:, who, kc, :], wst[wb][:, kc, :], kc == 0, kc == 7,
                                  ["scB", ("wst", wb)], [("ps", pb)])
                        part = cc // 2
                        cs = (cc % 2) * 512
                        bsl = bmb[:, cc * 512:(cc + 1) * 512]
                        if part == 0:
                            sc.tt("dve", modB[who][:, cs:cs + 512], pt, bsl, ALU.add,
                                  [("ps", pb), "bmb"], [("modB", who)])
                        elif part == 1:
                            sc.stt(modA[who][:, cs:cs + 512], pt, 1.0, bsl, ALU.add, ALU.add,
                                   [("ps", pb), "bmb"], [("modA", who)])
                            sc.tt("pool", modA[who][:, cs:cs + 512], modA[who][:, cs:cs + 512], npre[:, cs:cs + 512],
                                  ALU.mult, [("modA", who), "npre"], [("modA", who)])
                        else:
                            sc.tt("dve", modG[who][:, cs:cs + 512], pt, bsl, ALU.add,
                                  [("ps", pb), "bmb"], [("modG", who)])
                            sc.tt("pool", modG[who][:, cs:cs + 512], modG[who][:, cs:cs + 512], npost[:, cs:cs + 512],
                                  ALU.mult, [("modG", who), "npost"], [("modG", who)])
                sc.flush()

            if even:
                phaseA_even(nc, sc, l, j, src_x, w_in_e, ropec, ropes, ident, modA, modB, ps_all,
                            qT_d, kT_d, vx_d, sg_d, zT_d, wT_d)
                phaseB_even(nc, sc, l, j, last, src_x, xs, out, w_out, lam_a, subln_a, conv_b, ident, modG, ps_all,
                            qT_d, kT_d, vx_d, sg_d, zT_d, wT_d)
            else:
                phaseA_odd(nc, sc, l, j, src_x, w_in_o, cs64, ident, modA, modB, ps_all,
                           qT_d, kT_d, vx_d, sg_d, sg2_d, z1_d, z2_d)
                if KSTOP == "A":
                    break
                phaseB1_odd(nc, sc, l, last, dftc, dfts, dftc_l, dfts_l, ps_all, z1_d, z2_d, sg2_d, yg_d)
                if KSTOP == "B1":
                    break
                phaseB2_odd(nc, sc, l, j, last, src_x, xs, out, w_out, rpbT, ident, identb, modG, ps_all,
                            qT_d, kT_d, vx_d, sg_d, yg_d)
    return nc


def load_cast_weights(nc, sc, ph, Wb, w_ap, ncols, tag):
    CW = 256
    NB = 4
    wst = [ph.enter_context(nc.sbuf_tensor(un("%s_st%d" % (tag, i)), [128, 8, CW], F32)) for i in range(NB)]
    for cc in range(ncols // CW):
        b = sc.ring(tag + "st", NB)
        sc.dma(wst[b][:], w_ap[:, cc * CW:(cc + 1) * CW].rearrange("(k p) c -> p k c", p=128), (), [(tag + "st", b)])
        eng = "pool" if cc % 2 == 0 else "dve"
        sc.cp(eng, Wb[:, :, cc * CW:(cc + 1) * CW], wst[b][:], [(tag + "st", b)], [tag])


def norm_steps(nc, sc, src_x, t0, ntok, who, xt, junk, ss, rstd, tmp, hx, hxT, hb, ident, modA, modB, ps_all):
    nt = ntok // 128
    hbuf = {}

    def pre(ti):
        xb = sc.ring("xt", 2)
        r0 = t0 + ti * 128
        sc.dma(xt[xb][:], src_x[r0:r0 + 128, :], [("xs", r0 // 128)], [("xt", xb)])
        sc.act(junk[:], xt[xb][:], AF.Square, [("xt", xb)], ["junk", "ss"], accum_out=ss[:, 0:1])
        sc.ts("dve", rstd[:], ss[:, 0:1], 1.0 / D, EPS, ALU.mult, ALU.add, ["ss"], ["rstd"])
        sc.rsqrt(rstd[:], "rstd")
        sc.stt(tmp[:], xt[xb][:], rstd[:, 0:1], modA[who][:], ALU.mult, ALU.mult,
               [("xt", xb), "rstd", ("modA", who)], ["tmp"])
        hi = sc.ring("hx", 2)
        hbuf[ti] = hi
        sc.tt("pool", hx[hi][:], tmp[:], modB[who][:], ALU.add, ["tmp", ("modB", who)], [("hx", hi)])

    def tr(ti):
        hi = hbuf[ti]
        for kc in range(8):
            bank = 6 + kc // 4
            sc.tr(ps_all[:, bank, (kc % 4) * 128:(kc % 4 + 1) * 128], hx[hi][:, kc * 128:(kc + 1) * 128], ident[:],
                  [("hx", hi), "ident"], [("ps", bank)])
        for half in range(2):
            sc.cp("act", hxT[hb][:, half * 4:(half + 1) * 4, ti * 128:(ti + 1) * 128],
                  ps_all[:, 6 + half, :].rearrange("p (k t) -> p k t", k=4),
                  [("ps", 6 + half)], [("hxT", hb)])

    order = []
    for ti in range(nt):
        order.append(("pre", ti))
        if ti >= 1:
            order.append(("tr", ti - 1))
    order.append(("tr", nt - 1))
    return [(lambda k=k, ti=ti: pre(ti) if k == "pre" else tr(ti)) for k, ti in order]


class Ticker:
    def __init__(self, steps, total):
        self.steps = list(steps)
        self.n = len(self.steps)
        self.total = max(total, 1)
        self.t = 0
        self.done = 0

    def tick(self):
        self.t += 1
        want = min(self.n, (self.t * self.n + self.total - 1) // self.total)
        while self.done < want:
            self.steps[self.done]()
            self.done += 1

    def flush(self):
        while self.done < self.n:
            self.steps[self.done]()
            self.done += 1


def phaseA_even(nc, sc, l, j, src_x, w_in_e, ropec, ropes, ident, modA, modB, ps_all,
                qT_d, kT_d, vx_d, sg_d, zT_d, wT_d):
    with ExitStack() as ph:
        psb = lambda name, shape, dtype=F32: ph.enter_context(nc.sbuf_tensor(un(name), list(shape), dtype))
        Wb = psb("Wb", [128, 8, 5120], BF16)
        with ExitStack() as ph2:
            load_cast_weights(nc, sc, ph2, Wb, w_in_e[j], 5120, "Wb")
            sc.flush()
        xt = [psb("xt%d" % i, [128, D]) for i in range(2)]
        junk = psb("junk", [128, D])
        tmp = psb("tmp", [128, D])
        hx = [psb("hx%d" % i, [128, D]) for i in range(2)]
        ss = psb("ss", [128, 1])
        rstd = psb("rstd", [128, 1])
        hxT = [psb("hxT%d" % i, [128, 8, 512], BF16) for i in range(2)]
        rc = [psb("rc%d" % i, [128, 512]) for i in range(2)]
        rs_ = [psb("rs%d" % i, [128, 512]) for i in range(2)]
        t1 = [psb("t1_%d" % i, [128, 512]) for i in range(2)]
        t2 = [psb("t2_%d" % i, [128, 512]) for i in range(2)]
        qo = [psb("qo%d" % i, [128, 512], BF16) for i in range(3)]
        fo = [psb("fo%d" % i, [128, 512]) for i in range(3)]
        vxt = [psb("vxt%d" % i, [128, 8, 65], BF16) for i in range(2)]
        zpad = psb("zpad", [128, 4])
        for i in range(2):
            sc.memset("pool", vxt[i][:], 1.0, [("vxt", i)])
        sc.memset("pool", zpad[:], 0.0, ["zpad"])
        for dst in (zT_d, wT_d):
            for i4 in range(4):
                for col in (0, L + 1, ZC - 1):
                    sc.dma(dst[i4 * 128:(i4 + 1) * 128, col:col + 1], zpad[:, 0:1], ["zpad"], [], slow=True)

        chl = chunks_list()

        def mk_steps(ci, hb_):
            t0_, ntok_ = chl[ci]
            return norm_steps(nc, sc, src_x, t0_, ntok_, 0 if ci == 0 else 1, xt, junk, ss, rstd, tmp, hx, hxT, hb_,
                              ident, modA, modB, ps_all)

        hb_next = sc.ring("hxT", 2)
        for st in mk_steps(0, hb_next):
            st()
        for ci, (t0, ntok) in enumerate(chl):
            who = 0 if ci == 0 else 1
            hb = hb_next
            if ci + 1 < len(chl):
                hb_next = sc.ring("hxT", 2)
                tk = Ticker(mk_steps(ci + 1, hb_next), 40)
            else:
                tk = Ticker([], 1)
            rb = sc.ring("rope", 2)
            sc.dma(rc[rb][:, :ntok], ropec[:, t0:t0 + ntok], (), [("rc", rb)])
            sc.dma(rs_[rb][:, :ntok], ropes[:, t0:t0 + ntok], (), [("rs", rb)])

            def fm(col0):
                tk.tick()
                pb = sc.ring("psA", 6)
                for kc in range(8):
                    sc.mm(ps_all[:, pb, :ntok], Wb[:, kc, col0:col0 + 128], hxT[hb][:, kc, :ntok], kc == 0, kc == 7,
                          ["Wb", ("hxT", hb)], [("ps", pb)])
                return pb

            zc0 = t0 + 1 if ci == 0 else t0 + 2
            for which, dst in ((0, qT_d), (1, kT_d)):
                for i4 in range(4):
                    p1 = fm(which * 1024 + i4 * 128)
                    p2 = fm(which * 1024 + 512 + i4 * 128)
                    tb = sc.ring("t12", 2)
                    sc.tt("dve", t1[tb][:, :ntok], ps_all[:, p1, :ntok], rc[rb][:, :ntok], ALU.mult,
                          [("ps", p1), ("rc", rb)], [("t1", tb)])
                    sc.tt("dve", t2[tb][:, :ntok], ps_all[:, p2, :ntok], rs_[rb][:, :ntok], ALU.mult,
                          [("ps", p2), ("rs", rb)], [("t2", tb)])
                    ob = sc.ring("qo", 3)
                    sc.tt("pool", qo[ob][:, :ntok], t1[tb][:, :ntok], t2[tb][:, :ntok], ALU.add,
                          [("t1", tb), ("t2", tb)], [("qo", ob)])
                    sc.dma(dst[i4 * 128:(i4 + 1) * 128, t0:t0 + ntok], qo[ob][:, :ntok], [("qo", ob)], [])
            for i4 in range(4):
                pc_ = fm(3584 + i4 * 128)
                pu = fm(4096 + i4 * 128)
                tb = sc.ring("t12", 2)
                sc.cp("act", t1[tb][:, :ntok], ps_all[:, pc_, :ntok], [("ps", pc_)], [("t1", tb)])
                ob = sc.ring("fo", 3)
                sc.tt("dve", fo[ob][:, :ntok], ps_all[:, pu, :ntok], t1[tb][:, :ntok], ALU.mult,
                      [("ps", pu), ("t1", tb)], [("fo", ob)])
                sc.dma(zT_d[i4 * 128:(i4 + 1) * 128, zc0:zc0 + ntok], fo[ob][:, :ntok], [("fo", ob)], [])
                pbb = fm(3072 + i4 * 128)
                pg = fm(4608 + i4 * 128)
                sc.act(t2[tb][:, :ntok], ps_all[:, pg, :ntok], AF.Silu, [("ps", pg)], [("t2", tb)])
                ob = sc.ring("fo", 3)
                sc.tt("dve", fo[ob][:, :ntok], ps_all[:, pbb, :ntok], t2[tb][:, :ntok], ALU.mult,
                      [("ps", pbb), ("t2", tb)], [("fo", ob)])
                sc.dma(wT_d[i4 * 128:(i4 + 1) * 128, zc0:zc0 + ntok], fo[ob][:, :ntok], [("fo", ob)], [])
            for ti in range(ntok // 128):
                r0 = t0 + ti * 128
                pb = sc.ring("psA", 6)
                for kc in range(8):
                    sc.mm(ps_all[:, pb, :], hxT[hb][:, kc, ti * 128:(ti + 1) * 128], Wb[:, kc, 2048:2560],
                          kc == 0, kc == 7, ["Wb", ("hxT", hb)], [("ps", pb)])
                vb = sc.ring("vxt", 2)
                sc.cp("act", vxt[vb][:, :, 0:64], ps_all[:, pb, :].rearrange("p (h d) -> p h d", h=8),
                      [("ps", pb)], [("vxt", vb)])
                sc.dma(vx_d[r0:r0 + 128, :], vxt[vb][:].rearrange("p h d -> p (h d)"), [("vxt", vb)], [])
                pb = sc.ring("psA", 6)
                for kc in range(8):
                    sc.mm(ps_all[:, pb, :], hxT[hb][:, kc, ti * 128:(ti + 1) * 128], Wb[:, kc, 2560:3072],
                          kc == 0, kc == 7, ["Wb", ("hxT", hb)], [("ps", pb)])
                ob = sc.ring("fo", 3)
                sc.act(fo[ob][:], ps_all[:, pb, :], AF.Silu, [("ps", pb)], [("fo", ob)])
                sc.dma(sg_d[r0:r0 + 128, :], fo[ob][:], [("fo", ob)], [])
            tk.flush()
        sc.flush()


def out_proj_tile(nc, sc, l, last, src_x, xs, out, r0, mxT, tcol, Wo, modG, who, xt, junk, ss2, rstd, tmp, ps_all,
                  banks=(6, 7)):
    xb = sc.ring("xt", 2)
    sc.dma(xt[xb][:], src_x[r0:r0 + 128, :], [("xs", r0 // 128)], [("xt", xb)])
    for half in range(2):
        bank = banks[half]
        for kc in range(8):
            sc.mm(ps_all[:, bank, :], mxT[:, kc, tcol:tcol + 128], Wo[:, kc, half * 512:(half + 1) * 512],
                  kc == 0, kc == 7, ["mxT", "Wo"], [("ps", bank)])
        sc.act(junk[:, 0:512], ps_all[:, bank, :], AF.Square, [("ps", bank)], ["junk", ("ss2", half)],
               accum_out=ss2[:, half:half + 1])
    sc.tt("dve", rstd[:], ss2[:, 0:1], ss2[:, 1:2], ALU.add, [("ss2", 0), ("ss2", 1)], ["rstd"])
    sc.ts("dve", rstd[:], rstd[:], 1.0 / D, EPS, ALU.mult, ALU.add, ["rstd"], ["rstd"])
    sc.rsqrt(rstd[:], "rstd")
    for half in range(2):
        bank = banks[half]
        sc.stt(tmp[:, half * 512:(half + 1) * 512], ps_all[:, bank, :], rstd[:, 0:1],
               modG[who][:, half * 512:(half + 1) * 512], ALU.mult, ALU.mult,
               [("ps", bank), "rstd", ("modG", who)], [("tmp", half)])
    sc.tt("pool", xt[xb][:], xt[xb][:], tmp[:], ALU.add, [("xt", xb), ("tmp", 0), ("tmp", 1)], [("xt", xb)])
    if last:
        sc.dma(out[r0 - L:r0 - L + 128, :], xt[xb][:], [("xt", xb)], [], q="pool")
    else:
        sc.dma(xs[r0:r0 + 128, :], xt[xb][:], [("xt", xb)], [("xs", r0 // 128)], q="pool")


def phaseB_even(nc, sc, l, j, last, src_x, xs, out, w_out, lam_a, subln_a, conv_b, ident, modG, ps_all,
                qT_d, kT_d, vx_d, sg_d, zT_d, wT_d):
    scale = 32 ** -0.5
    li = lambda_init(l)
    with ExitStack() as ph:
        psb = lambda name, shape, dtype=F32: ph.enter_context(nc.sbuf_tensor(un(name), list(shape), dtype))
        Wo = psb("Wo", [128, 8, D], BF16)
        kT = psb("kT", [128, 4, T], BF16)
        V = psb("V", [128, NT, 520], BF16)
        lam_t = psb("lam_t", [128, 128])
        lam_p = psb("lam_p", [128, 64])
        lam_s = psb("lam_s", [128, 4])
        neglam = psb("neglam", [128, 1])
        subl = psb("subl", [128, 8, 64])
        cw = psb("cw", [128, 4, 3])
        with ExitStack() as ph2:
            load_cast_weights(nc, sc, ph2, Wo, w_out[l], D, "Wo")
            for i4 in range(4):
                sc.dma(kT[:, i4, :], kT_d[i4 * 128:(i4 + 1) * 128, :], (), ["kT"])
            for g in range(2):
                sc.dma(V[:, g * 17:(g + 1) * 17, :],
                       vx_d[g * 17 * 128:(g + 1) * 17 * 128, :].rearrange("(t p) c -> p t c", p=128), (), ["V"])
            sc.dma(lam_t[:], lam_a[j, :].partition_broadcast(128), (), ["lam_t"])
            lv = lam_t[:].rearrange("p (a b c) -> p a b c", a=2, b=2)
            sc.tt("dve", lam_p[:].rearrange("p (a c) -> p a c", a=2), lv[:, :, 0, :], lv[:, :, 1, :], ALU.mult,
                  ["lam_t"], ["lam_p"])
            sc.red(lam_s[:, 0:2], lam_p[:].rearrange("p (a c) -> p a c", a=2), ["lam_p"], ["lam_s"])
            sc.act(lam_s[:, 2:4], lam_s[:, 0:2], AF.Exp, ["lam_s"], ["lam_s"])
            sc.tt("dve", neglam[:], lam_s[:, 3:4], lam_s[:, 2:3], ALU.subtract, ["lam_s"], ["neglam"])
            sc.ts("dve", neglam[:], neglam[:], -li, None, ALU.add, None, ["neglam"], ["neglam"])
            for h in range(8):
                sc.dma(subl[:, h, :], subln_a[j, :].partition_broadcast(128), (), ["subl"])
            sc.ts("dve", subl[:], subl[:], 1.0 - li, None, ALU.mult, None, ["subl"], ["subl"])
            sc.dma(cw[:].rearrange("p a k -> p (a k)"), conv_b[j, :, :], (), ["cw"])
            sc.flush()
        qT = [psb("qT%d" % i, [128, 4, 512], BF16) for i in range(2)]
        P = [psb("P%d" % i, [128, 2, 512], BF16) for i in range(4)]
        osb = psb("osb", [65, 4, 512])
        O = [psb("O%d" % i, [128, 4, 2, 65]) for i in range(2)]
        R = psb("R", [128, 16])
        Dd = psb("Dd", [128, 4, 2, 64])
        D1 = psb("D1", [128, 4, 2, 64])
        sq = psb("sq", [128, 4, 2, 64])
        ssh = psb("ssh", [128, 4, 2])
        sgc = psb("sgc", [128, 4, 512])
        Yc = psb("Yc", [128, 4, 512])
        mxT = psb("mxT", [128, 8, 512], BF16)
        zc = [psb("zc%d" % i, [128, 514]) for i in range(2)]
        wc = [psb("wc%d" % i, [128, 512]) for i in range(2)]
        yc = psb("yc", [128, 512])
        xt = [psb("xt%d" % i, [128, D]) for i in range(2)]
        junk = psb("junk", [128, 512])
        tmp = psb("tmp", [128, D])
        ss2 = psb("ss2", [128, 2])
        rstd = psb("rstd", [128, 1])

        chl_b = chunks_list()
        qpre = {}
        for ci, (t0, nq) in enumerate(chl_b):
            if ci == 0 and last:
                continue
            who = 0 if ci == 0 else 1
            kts = [0, 1] if ci == 0 else list(range(NT))
            nqt = nq // 128
            if ci in qpre:
                qb = qpre.pop(ci)
            else:
                qb = sc.ring("qT", 2)
                for i4 in range(4):
                    sc.dma(qT[qb][:, i4, :nq], qT_d[i4 * 128:(i4 + 1) * 128, t0:t0 + nq], (), [("qT", qb)])
            for qt in range(nqt):
                sc.dma(sgc[:, qt, :], sg_d[t0 + qt * 128:t0 + (qt + 1) * 128, :], (), ["sgc"])
            if ci + 1 < len(chl_b):
                t0n, nqn = chl_b[ci + 1]
                qbn = sc.ring("qT", 2)
                for i4 in range(4):
                    sc.dma(qT[qbn][:, i4, :nqn], qT_d[i4 * 128:(i4 + 1) * 128, t0n:t0n + nqn], (), [("qT", qbn)])
                qpre[ci + 1] = qbn
            zc0 = t0 + 1 if ci == 0 else t0 + 2
            for i4 in range(4):
                zb = sc.ring("zc", 2)
                sc.dma(zc[zb][:, :nq + 2], zT_d[i4 * 128:(i4 + 1) * 128, zc0 - 1:zc0 + nq + 1], (), [("zc", zb)])
                sc.dma(wc[zb][:, :nq], wT_d[i4 * 128:(i4 + 1) * 128, zc0:zc0 + nq], (), [("wc", zb)])
                sc.ts("dve", yc[:, :nq], zc[zb][:, 0:nq], cw[:, i4, 0:1], None, ALU.mult, None,
                      [("zc", zb), "cw"], ["yc"])
                sc.stt(yc[:, :nq], zc[zb][:, 1:nq + 1], cw[:, i4, 1:2], yc[:, :nq], ALU.mult, ALU.add,
                       [("zc", zb), "cw", "yc"], ["yc"])
                sc.stt(yc[:, :nq], zc[zb][:, 2:nq + 2], cw[:, i4, 2:3], yc[:, :nq], ALU.mult, ALU.add,
                       [("zc", zb), "cw", "yc"], ["yc"])
                sc.tt("pool", mxT[:, 4 + i4, :nq], yc[:, :nq], wc[zb][:, :nq], ALU.mult,
                      ["yc", ("wc", zb)], ["mxT"])
            its = [(pr, ki, kt) for pr in range(4) for ki, kt in enumerate(kts)]

            def emit_S(n, half):
                pr, ki, kt = its[n]
                for m in range(2):
                    p0 = 64 * half + 32 * m
                    bank = 2 * half + m
                    sc.mm(ps_all[:, bank, :nq], kT[p0:p0 + 32, pr, kt * 128:(kt + 1) * 128],
                          qT[qb][p0:p0 + 32, pr, :nq], True, True, ["kT", ("qT", qb)], [("ps", bank)],
                          tp=(p0, 0))

            def emit_PV(n, half, pbuf):
                pr, ki, kt = its[n]
                h = 2 * pr + half
                for m in range(2):
                    bank = 4 + 2 * half + m
                    sc.mm(ps_all[0:65, bank, :nq], V[:, kt, h * 65:(h + 1) * 65], P[pbuf][:, m, :nq],
                          ki == 0, ki == len(kts) - 1, ["V", ("P", pbuf)], [("ps", bank)])

            emit_S(0, 0)
            emit_S(0, 1)
            pending = []
            for n, (pr, ki, kt) in enumerate(its):
                if pending and n >= pending[0][0]:
                    pending.pop(0)[1]()
                pbs = []
                for half in range(2):
                    pbuf = sc.ring("P", 4)
                    pbs.append(pbuf)
                    sc.act(P[pbuf][:, :, :nq], ps_all[:, 2 * half:2 * half + 2, :nq], AF.Exp,
                           [("ps", 2 * half), ("ps", 2 * half + 1)], [("P", pbuf)], scale=scale)
                if n + 1 < len(its):
                    emit_S(n + 1, 0)
                    emit_S(n + 1, 1)
                emit_PV(n, 0, pbs[0])
                emit_PV(n, 1, pbs[1])
                if ki != len(kts) - 1:
                    continue
                sc.cp("dve", osb[:, :, :nq], ps_all[0:65, 4:8, :nq], [("ps", 4), ("ps", 5), ("ps", 6), ("ps", 7)],
                      ["osb"])
                for jj in range(4):
                    half, m = jj // 2, jj % 2
                    h = 2 * pr + half
                    for qt in range(nqt):
                        sc.tr(ps_all[:, 4 + jj, qt * 65:(qt + 1) * 65], osb[:, jj, qt * 128:(qt + 1) * 128],
                              ident[0:65, 0:65], ["osb", "ident"], [("ps", 4 + jj)])
                    sc.cp("dve", O[m][:, 0:nqt, half, :],
                          ps_all[:, 4 + jj, 0:nqt * 65].rearrange("p (q d) -> p q d", d=65),
                          [("ps", 4 + jj)], [("O", m)])
                while pending:
                    pending.pop(0)[1]()
                Q = slice(0, nqt)
                R4 = R[:, 0:2 * nqt].rearrange("p (q h) -> p q h", h=2)
                R4b = R[:, 8:8 + 2 * nqt].rearrange("p (q h) -> p q h", h=2)
                sc.recip(R4, O[0][:, Q, :, 64], [("O", 0)], ["R"])
                sc.recip(R4b, O[1][:, Q, :, 64], [("O", 1)], ["R"])
                sc.ts("dve", R4b, R4b, neglam[:, 0:1], None, ALU.mult, None, ["R", "neglam"], ["R"])
                sc.tt("dve", Dd[:, Q], O[0][:, Q, :, 0:64], R4.unsqueeze(3).to_broadcast([128, nqt, 2, 64]),
                      ALU.mult, [("O", 0), "R"], ["Dd"])
                sc.tt("pool", D1[:, Q], O[1][:, Q, :, 0:64], R4b.unsqueeze(3).to_broadcast([128, nqt, 2, 64]),
                      ALU.mult, [("O", 1), "R"], ["D1"])
                sc.tt("dve", Dd[:, Q], Dd[:, Q], D1[:, Q], ALU.add, ["Dd", "D1"], ["Dd"])
                sc.tt("pool", sq[:, Q], Dd[:, Q], Dd[:, Q], ALU.mult, ["Dd"], ["sq"])
                sc.red(ssh[:, Q, :], sq[:, Q], ["sq"], ["ssh"])
                sc.ts("dve", ssh[:, Q, :], ssh[:, Q, :], 1.0 / 64, EPS, ALU.mult, ALU.add, ["ssh"], ["ssh"])

                def stage2(pr=pr):
                    sc.rsqrt(ssh[:, Q, :], "ssh")
                    sc.tt("dve", Dd[:, Q], Dd[:, Q], ssh[:, Q, :].unsqueeze(3).to_broadcast([128, nqt, 2, 64]),
                          ALU.mult, ["Dd", "ssh"], ["Dd"])
                    sc.tt("pool", Dd[:, Q], Dd[:, Q], subl[:, 0:2, :].unsqueeze(1).to_broadcast([128, nqt, 2, 64]),
                          ALU.mult, ["Dd", "subl"], ["Dd"])
                    sc.tt("pool", Yc[:, Q, pr * 128:(pr + 1) * 128], Dd[:, Q].rearrange("p q h d -> p q (h d)"),
                          sgc[:, Q, pr * 128:(pr + 1) * 128], ALU.mult, ["Dd", "sgc"], ["Yc"])

                pending.append((n + 12, stage2))
            while pending:
                pending.pop(0)[1]()
            for qt in range(nqt):
                for i4 in range(4):
                    sc.tr(ps_all[:, 6, i4 * 128:(i4 + 1) * 128], Yc[:, qt, i4 * 128:(i4 + 1) * 128], ident[:],
                          ["Yc", "ident"], [("ps", 6)])
                sc.cp("act", mxT[:, 0:4, qt * 128:(qt + 1) * 128],
                      ps_all[:, 6, :].rearrange("p (k t) -> p k t", k=4), [("ps", 6)], ["mxT"])
            for qt in range(nqt):
                out_proj_tile(nc, sc, l, last, src_x, xs, out, t0 + qt * 128, mxT, qt * 128, Wo, modG, who,
                              xt, junk, ss2, rstd, tmp, ps_all, banks=(4, 5) if qt % 2 == 0 else (6, 7))
        sc.flush()


def phaseA_odd(nc, sc, l, j, src_x, w_in_o, cs64, ident, modA, modB, ps_all,
               qT_d, kT_d, vx_d, sg_d, sg2_d, z1_d, z2_d):
    with ExitStack() as ph:
        psb = lambda name, shape, dtype=F32: ph.enter_context(nc.sbuf_tensor(un(name), list(shape), dtype))
        Wb = psb("Wbo", [128, 8, 3072], BF16)
        csb = psb("csb", [128, 256], BF16)
        with ExitStack() as ph2:
            load_cast_weights(nc, sc, ph2, Wb, w_in_o[j], 3072, "Wb")
            sc.dma(csb[:], cs64[:, :], (), ["csb"])
            sc.flush()
        xt = [psb("xt%d" % i, [128, D]) for i in range(2)]
        junk = psb("junk", [128, D])
        tmp = psb("tmp", [128, D])
        hx = [psb("hx%d" % i, [128, D]) for i in range(2)]
        ss = psb("ss", [128, 1])
        rstd = psb("rstd", [128, 1])
        hxT = [psb("hxT%d" % i, [128, 8, 512], BF16) for i in range(2)]
        qo = [psb("qo%d" % i, [128, 512], BF16) for i in range(3)]
        fo = [psb("fo%d" % i, [128, 512]) for i in range(3)]
        zo = [psb("zo%d" % i, [128, 512], BF16) for i in range(3)]
        uT = psb("uT", [128, 4, 512], BF16)
        vxt = [psb("vxt%d" % i, [128, 8, 65], BF16) for i in range(2)]
        for i in range(2):
            sc.memset("pool", vxt[i][:], 1.0, [("vxt", i)])

        chl = chunks_list()

        def mk_steps(ci, hb_):
            t0_, ntok_ = chl[ci]
            return norm_steps(nc, sc, src_x, t0_, ntok_, 0 if ci == 0 else 1, xt, junk, ss, rstd, tmp, hx, hxT, hb_,
                              ident, modA, modB, ps_all)

        hb_next = sc.ring("hxT", 2)
        for st in mk_steps(0, hb_next):
            st()
        for ci, (t0, ntok) in enumerate(chl):
            who = 0 if ci == 0 else 1
            hb = hb_next
            if ci + 1 < len(chl):
                hb_next = sc.ring("hxT", 2)
                tk = Ticker(mk_steps(ci + 1, hb_next), 24)
            else:
                tk = Ticker([], 1)

            def fm(col0):
                tk.tick()
                pb = sc.ring("psA", 6)
                for kc in range(8):
                    sc.mm(ps_all[:, pb, :ntok], Wb[:, kc, col0:col0 + 128], hxT[hb][:, kc, :ntok], kc == 0, kc == 7,
                          ["Wb", ("hxT", hb)], [("ps", pb)])
                return pb

            for which, dst in ((0, qT_d), (1, kT_d)):
                for i4 in range(4):
                    p1 = fm(which * 512 + i4 * 128)
                    ob = sc.ring("qo", 3)
                    sc.cp("act" if i4 % 2 == 0 else "dve", qo[ob][:, :ntok], ps_all[:, p1, :ntok],
                          [("ps", p1)], [("qo", ob)])
                    sc.dma(dst[i4 * 128:(i4 + 1) * 128, t0:t0 + ntok], qo[ob][:, :ntok], [("qo", ob)], [])
            for i4 in range(4):
                p1 = fm(2048 + i4 * 128)
                sc.cp("act" if i4 % 2 == 0 else "dve", uT[:, i4, :ntok], ps_all[:, p1, :ntok], [("ps", p1)], ["uT"])

            def tm(c0):
                tk.tick()
                pb = sc.ring("psA", 6)
                for kc in range(8):
                    sc.mm(ps_all[:, pb, :], hxT[hb][:, kc, ti * 128:(ti + 1) * 128], Wb[:, kc, c0:c0 + 512],
                          kc == 0, kc == 7, ["Wb", ("hxT", hb)], [("ps", pb)])
                return pb

            for ti in range(ntok // 128):
                r0 = t0 + ti * 128
                pb = tm(1024)
                vb = sc.ring("vxt", 2)
                sc.cp("act", vxt[vb][:, :, 0:64], ps_all[:, pb, :].rearrange("p (h d) -> p h d", h=8),
                      [("ps", pb)], [("vxt", vb)])
                sc.dma(vx_d[r0:r0 + 128, :], vxt[vb][:].rearrange("p h d -> p (h d)"), [("vxt", vb)], [])
                for c0, dst in ((1536, sg_d), (2560, sg2_d)):
                    pb = tm(c0)
                    ob = sc.ring("fo", 3)
                    sc.act(fo[ob][:], ps_all[:, pb, :], AF.Silu, [("ps", pb)], [("fo", ob)])
                    sc.dma(dst[r0:r0 + 128, :], fo[ob][:], [("fo", ob)], [])
                for which, dst in ((0, z1_d), (1, z2_d)):
                    pb = sc.ring("psA", 6)
                    for i4 in range(4):
                        sc.mm(ps_all[:, pb, i4 * 128:(i4 + 1) * 128], uT[:, i4, ti * 128:(ti + 1) * 128],
                              csb[:, which * 128:(which + 1) * 128], True, True, ["uT", "csb"], [("ps", pb)])
                    ob = sc.ring("zo", 3)
                    sc.cp("dve", zo[ob][:], ps_all[:, pb, :], [("ps", pb)], [("zo", ob)])
                    sc.dma(dst[r0:r0 + 128, :], zo[ob][:], [("zo", ob)], [])
            tk.flush()
        sc.flush()


def phaseB1_odd(nc, sc, l, last, dftc, dfts, dftc_l, dfts_l, ps_all, z1_d, z2_d, sg2_d, yg_d):
    with ExitStack() as ph:
        psb = lambda name, shape, dtype=F32: ph.enter_context(nc.sbuf_tensor(un(name), list(shape), dtype))
        Z = [psb("Z%d" % i, [128, 32, 512], BF16) for i in range(2)]
        Zc = [psb("Zc%d" % i, [128, 2, 512], BF16) for i in range(2)]
        Cc = [psb("Cc%d" % i, [128, 32 * 256], BF16) for i in range(2)]
        Sc = [psb("Sc%d" % i, [128, 32 * 256], BF16) for i in range(2)]
        Cl = psb("Cl", [128, 512], BF16)
        Sl = psb("Sl", [128, 512], BF16)
        sgd = [psb("sgd%d" % i, [128, 512]) for i in range(2)]
        yo = [psb("yo%d" % i, [128, 512]) for i in range(2)]
        for i, zd in enumerate((z1_d, z2_d)):
            for g in range(4):
                sc.dma(Z[i][:, g * 8:(g + 1) * 8, :],
                       zd[L + g * 1024:L + (g + 1) * 1024, :].rearrange("(t p) c -> p t c", p=128), (), [("Z", i)])
            sc.dma(Zc[i][:], zd[0:L, :].rearrange("(t p) c -> p t c", p=128), (), [("Zc", i)])
        sc.dma(Cl[:], dftc_l[:, :], (), ["Cl"])
        sc.dma(Sl[:], dfts_l[:, :], (), ["Sl"])

        def finish(pb, r0):
            gb = sc.ring("sgd", 2)
            sc.dma(sgd[gb][:], sg2_d[r0:r0 + 128, :], (), [("sgd", gb)])
            sc.tt("dve", yo[gb][:], ps_all[:, pb, :], sgd[gb][:], ALU.mult, [("ps", pb), ("sgd", gb)], [("yo", gb)])
            sc.dma(yg_d[r0:r0 + 128, :], yo[gb][:], [("yo", gb)], [], q="pool")

        if not last:
            for tt_ in range(2):
                pb = sc.ring("psF", 4)
                n = 0
                for (Mt, zi) in ((Cl, 0), (Sl, 1)):
                    for nt in range(2):
                        sc.mm(ps_all[:, pb, :], Mt[:, nt * 256 + tt_ * 128:nt * 256 + (tt_ + 1) * 128], Zc[zi][:, nt, :],
                              n == 0, n == 3, ["Cl", "Sl", ("Zc", zi)], [("ps", pb)])
                        n += 1
                finish(pb, tt_ * 128)
        for ch in range(16):
            cb = sc.ring("dft", 2)
            sc.dma(Cc[cb][:], dftc[ch, :, :], (), [("Cc", cb)])
            sc.dma(Sc[cb][:], dfts[ch, :, :], (), [("Sc", cb)])
            for tt_ in range(2):
                pb = sc.ring("psF", 4)
                n = 0
                for (Mt, key, zi) in ((Cc[cb], ("Cc", cb), 0), (Sc[cb], ("Sc", cb), 1)):
                    for nt in range(32):
                        sc.mm(ps_all[:, pb, :], Mt[:, nt * 256 + tt_ * 128:nt * 256 + (tt_ + 1) * 128], Z[zi][:, nt, :],
                              n == 0, n == 63, [key, ("Z", zi)], [("ps", pb)])
                        n += 1
                finish(pb, L + ch * 256 + tt_ * 128)
        sc.flush()


def phaseB2_odd(nc, sc, l, j, last, src_x, xs, out, w_out, rpbT, ident, identb, modG, ps_all,
                qT_d, kT_d, vx_d, sg_d, yg_d):
    scale = 0.125
    with ExitStack() as ph:
        psb = lambda name, shape, dtype=F32: ph.enter_context(nc.sbuf_tensor(un(name), list(shape), dtype))
        Wo = psb("Wo", [128, 8, D], BF16)
        Tb = psb("Tb", [128, 8, 960], BF16)
        kTc = psb("kTc", [128, 4, L], BF16)
        Vc = psb("Vc", [128, 2, 520], BF16)
        with ExitStack() as ph2:
            load_cast_weights(nc, sc, ph2, Wo, w_out[l], D, "Wo")
            tst = [ph2.enter_context(nc.sbuf_tensor(un("tst%d" % i), [128, 896], F32)) for i in range(2)]
            for h in range(8):
                tb = sc.ring("tst", 2)
                sc.dma(tst[tb][0:64, :], rpbT[j, h, :, 0:896], (), [("tst", tb)])
                sc.dma(tst[tb][64:128, :], rpbT[j, h, :, 64:960], (), [("tst", tb)])
                sc.act(Tb[:, h, 0:896], tst[tb][:], AF.Exp, [("tst", tb)], ["Tb"])
            for i4 in range(4):
                sc.dma(kTc[:, i4, :], kT_d[i4 * 128:(i4 + 1) * 128, 0:L], (), ["kTc"])
            sc.dma(Vc[:], vx_d[0:L, :].rearrange("(t p) c -> p t c", p=128), (), ["Vc"])
            sc.flush()
        kTw = [psb("kTw%d" % i, [128, 4, 1024], BF16) for i in range(2)]
        Vw0 = [psb("Vw0_%d" % i, [128, 8, 520], BF16) for i in range(2)]
        Vw1 = [psb("Vw1_%d" % i, [128, 7, 520], BF16) for i in range(2)]
        qT = [psb("qT%d" % i, [128, 4, 512], BF16) for i in range(2)]
        P = [psb("P%d" % i, [128, 2, 384], BF16) for i in range(3)]
        R = psb("R", [64, 8])
        Yo = psb("Yo", [64, 512])
        Y = [psb("Y%d" % i, [64, 512]) for i in range(2)]
        sgr = [psb("sgr%d" % i, [64, 512]) for i in range(2)]
        yg = [psb("yg%d" % i, [128, 512]) for i in range(2)]
        mxT = psb("mxT", [128, 8, 512], BF16)
        xt = [psb("xt%d" % i, [128, D]) for i in range(2)]
        junk = psb("junk", [128, 512])
        tmp = psb("tmp", [128, D])
        ss2 = psb("ss2", [128, 2])
        rstd = psb("rstd", [128, 1])
        obanks = (4, 5)

        for ci, (t0, nq) in enumerate(chunks_list()):
            if ci == 0 and last:
                continue
            who = 0 if ci == 0 else 1
            window = ci > 0
            nrow = nq // 64
            qb = sc.ring("qT", 2)
            for i4 in range(4):
                sc.dma(qT[qb][:, i4, :nq], qT_d[i4 * 128:(i4 + 1) * 128, t0:t0 + nq], (), [("qT", qb)])
            wb = 0
            wlo = 0
            if window:
                c = ci - 1
                lo = max(8 * c - 4, 0)
                wlo = min(lo, 48)
                wb = sc.ring("win", 2)
                for i4 in range(4):
                    sc.dma(kTw[wb][:, i4, :], kT_d[i4 * 128:(i4 + 1) * 128, L + wlo * 64:L + wlo * 64 + 1024], (),
                           [("kTw", wb)])
                sc.dma(Vw0[wb][:], vx_d[L + wlo * 64:L + wlo * 64 + 1024, :].rearrange("(t p) c -> p t c", p=128), (),
                       [("Vw", wb)])
                sc.dma(Vw1[wb][:], vx_d[L + (wlo + 1) * 64:L + (wlo + 1) * 64 + 896, :].rearrange(
                    "(t p) c -> p t c", p=128), (), [("Vw", wb)])
            NW = 256 if window else 0

            def win_params(qi):
                if not window:
                    return 0, 0
                r = 8 * (ci - 1) + qi
                rs = min(max(r - 4, 0), 56)
                return rs, rs - r + 7

            its = [(qi, pr) for qi in range(nrow) for pr in range(4)]
            slot = {}

            def emit_S(n):
                qi, pr = its[n]
                rs, a0 = win_params(qi)
                s_ = sc.ring("S", 2)
                slot[n] = s_
                skey = ("S", s_)
                qs = [qT[qb][half * 64:(half + 1) * 64, pr, qi * 64:(qi + 1) * 64] for half in range(2)]
                if window:
                    for p in range(4):
                        woff = (rs + 2 * p - wlo) * 64
                        for half in range(2):
                            pb0 = half * 64
                            sc.mm(ps_all[:, 2 * s_ + half, p * 64:(p + 1) * 64],
                                  kTw[wb][pb0:pb0 + 64, pr, woff:woff + 128], qs[half], True, True,
                                  [("kTw", wb), ("qT", qb)], [skey], tp=(pb0, 0))
                for t in range(2):
                    for half in range(2):
                        pb0 = half * 64
                        sc.mm(ps_all[:, 2 * s_ + half, NW + t * 64:NW + (t + 1) * 64],
                              kTc[pb0:pb0 + 64, pr, t * 128:(t + 1) * 128], qs[half], True, True,
                              ["kTc", ("qT", qb)], [skey], tp=(pb0, 0))

            emit_S(0)
            pending = []
            for n, (qi, pr) in enumerate(its):
                if n + 1 < len(its):
                    emit_S(n + 1)
                if pending and n >= pending[0][0]:
                    pending.pop(0)[1]()
                rs, a0 = win_params(qi)
                s_ = slot.pop(n)
                skey = ("S", s_)
                pbuf = sc.ring("P", 3)
                ncol = NW + 128
                sc.act(P[pbuf][:, :, 0:ncol], ps_all[:, 2 * s_:2 * s_ + 2, 0:ncol], AF.Exp, [skey],
                       [("P", pbuf, 0), ("P", pbuf, 1)], scale=scale)
                if window:
                    for half in range(2):
                        h = 2 * pr + half
                        eb = Tb[:, h, a0 * 64:a0 * 64 + 512].rearrange("p (a two c) -> p a two c", two=2, c=64)[:, :, 0, :]
                        pv = P[pbuf][:, half, 0:256].rearrange("p (a c) -> p a c", c=64)
                        sc.tt("dve", pv, pv, eb, ALU.mult, [("P", pbuf, half), "Tb"],
                              [("P", pbuf, half)])
                par = (rs - wlo) % 2
                pidx0 = (rs - wlo - par) // 2
                Vw = Vw1[wb] if par else Vw0[wb]
                obanks = (4, 5) if qi % 2 == 0 else (6, 7)
                for half in range(2):
                    h = 2 * pr + half
                    ob = obanks[h // 4]
                    o_ap = ps_all[0:64, ob, (h % 4) * 65:(h % 4 + 1) * 65]
                    first = True
                    if window:
                        for p in range(4):
                            sc.mm(o_ap, P[pbuf][:, half, p * 64:(p + 1) * 64], Vw[:, pidx0 + p, h * 65:(h + 1) * 65],
                                  first, False, [("P", pbuf, half), ("Vw", wb)], [("ps", ob)])
                            first = False
                    for t in range(2):
                        sc.mm(o_ap, P[pbuf][:, half, NW + t * 64:NW + (t + 1) * 64], Vc[:, t, h * 65:(h + 1) * 65],
                              first, t == 1, [("P", pbuf, half), "Vc"], [("ps", ob)])
                        first = False
                if pr != 3:
                    continue
                rr0 = t0 + qi * 64
                gb = sc.ring("sgr", 2)
                sc.dma(sgr[gb][:], sg_d[rr0:rr0 + 64, :], (), [("sgr", gb)])
                for half in range(2):
                    ov = ps_all[0:64, obanks[half], 0:260].rearrange("p (h d) -> p h d", d=65)
                    sc.recip(R[:, half * 4:(half + 1) * 4], ov[:, :, 64], [("ps", obanks[half])], ["R"])
                    sc.tt("dve", Yo[:, half * 256:(half + 1) * 256].rearrange("p (h d) -> p h d", d=64), ov[:, :, 0:64],
                          R[:, half * 4:(half + 1) * 4].unsqueeze(2).to_broadcast([64, 4, 64]), ALU.mult,
                          [("ps", obanks[half]), "R"], ["Yo"])
                yb_ = sc.ring("Yrow", 2)
                sc.tt("pool", Y[yb_][:], Yo[:], sgr[gb][:], ALU.mult, ["Yo", ("sgr", gb)], [("Y", yb_)])

                def row_tr(qi=qi, tbk=obanks[0], yb_=yb_):
                    for i4 in range(4):
                        sc.tr(ps_all[:, tbk, i4 * 64:(i4 + 1) * 64], Y[yb_][:, i4 * 128:(i4 + 1) * 128],
                              ident[0:64, 0:64], [("Y", yb_), "ident"], [("ps", tbk)])
                    sc.cp("dve", mxT[:, 0:4, qi * 64:(qi + 1) * 64],
                          ps_all[:, tbk, 0:256].rearrange("p (k t) -> p k t", k=4), [("ps", tbk)], ["mxT"])

                pending.append((n + 3, row_tr))
            while pending:
                pending.pop(0)[1]()
            for qt in range(nq // 128):
                r0 = t0 + qt * 128
                yb = sc.ring("yg", 2)
                sc.dma(yg[yb][:], yg_d[r0:r0 + 128, :], (), [("yg", yb)])
                for i4 in range(4):
                    sc.tr(ps_all[:, 7, i4 * 128:(i4 + 1) * 128], yg[yb][:, i4 * 128:(i4 + 1) * 128], ident[:],
                          [("yg", yb), "ident"], [("ps", 7)])
                sc.cp("act", mxT[:, 4:8, qt * 128:(qt + 1) * 128],
                      ps_all[:, 7, :].rearrange("p (k t) -> p k t", k=4), [("ps", 7)], ["mxT"])
            for qt in range(nq // 128):
                out_proj_tile(nc, sc, l, last, src_x, xs, out, t0 + qt * 128, mxT, qt * 128, Wo, modG, who,
                              xt, junk, ss2, rstd, tmp, ps_all, banks=(4, 5) if qt % 2 == 0 else (6, 7))
        sc.flush()


def host_constants():
    f32 = np.float32
    nf = 8
    inv_freq = (np.float32(10000.0) ** (-np.arange(nf, dtype=f32) / nf)).astype(f32)
    t = np.arange(S)
    row = (t // GW).astype(f32)
    col = (t % GW).astype(f32)
    ropec = np.ones((128, T), f32)
    ropes = np.zeros((128, T), f32)
    for p in range(128):
        d = p % 32
        pos = row if d < 16 else col
        dd = d % 16
        f = dd % 8
        ang = (pos * inv_freq[f]).astype(f32)
        ropec[p, L:] = np.cos(ang).astype(f32)
        sn = np.sin(ang).astype(f32)
        ropes[p, L:] = -sn if dd < 8 else sn
    identf = np.eye(128, dtype=f32)
    bf = ml_dtypes.bfloat16
    n = np.arange(S, dtype=np.int64)
    ang = (2.0 * np.pi / S) * ((n[:, None] * n[None, :]) % S).astype(np.float64)
    CN = (np.cos(ang) / 64.0).astype(f32)
    SN = (-np.sin(ang) / 64.0).astype(f32)

    def tile_dft(M):
        A = M.reshape(32, 128, 16, 256)
        return np.ascontiguousarray(A.transpose(2, 1, 0, 3).reshape(16, 128, 32 * 256)).astype(bf)

    dftc = tile_dft(CN)
    dfts = tile_dft(SN)
    n2 = np.arange(L, dtype=np.int64)
    ang2 = (2.0 * np.pi / L) * ((n2[:, None] * n2[None, :]) % L).astype(np.float64)
    CL = (np.cos(ang2) / 16.0).astype(f32).reshape(2, 128, 256).transpose(1, 0, 2).reshape(128, 512)
    SL = (-np.sin(ang2) / 16.0).astype(f32).reshape(2, 128, 256).transpose(1, 0, 2).reshape(128, 512)
    c64 = np.arange(64, dtype=np.int64)
    ang3 = (2.0 * np.pi / 64) * ((c64[:, None] * c64[None, :]) % 64).astype(np.float64)
    C64 = np.cos(ang3) / 8.0
    S64 = np.sin(ang3) / 8.0
    cs64 = np.zeros((128, 256), f32)
    for g in range(2):
        cs64[g * 64:(g + 1) * 64, g * 64:(g + 1) * 64] = C64
        cs64[g * 64:(g + 1) * 64, 128 + g * 64:128 + (g + 1) * 64] = S64
    return dict(ropec=ropec, ropes=ropes, identf=identf, dftc=dftc, dfts=dfts,
                dftc_l=np.ascontiguousarray(CL).astype(bf), dfts_l=np.ascontiguousarray(SL).astype(bf),
                cs64=cs64.astype(bf))


def host_layout(inputs):
    f32 = np.float32
    w_in_even = np.asarray(inputs["w_in_even"], f32)
    idx = np.arange(512)
    d = idx % 16
    sw = np.where(d < 8, idx + 8, idx - 8)
    q = w_in_even[:, :, 0:512]
    k = w_in_even[:, :, 512:1024]
    rest = w_in_even[:, :, 1024:]
    w_in_e = np.ascontiguousarray(np.concatenate([q, q[:, :, sw], k, k[:, :, sw], rest], axis=2))
    rpb = np.asarray(inputs["rpb_c"], f32)
    cq = np.arange(GW)
    cs = np.clip(cq - 8, 0, GW - 16)
    cp = np.arange(GW)
    inwin = (cp[:, None] >= cs[None, :]) & (cp[:, None] < cs[None, :] + 16)
    coff = np.clip(cp[:, None] - cq[None, :] + 15, 0, 30)
    g = rpb[:, :, :, coff]
    g = np.where(inwin[None, None, None], g, f32(NEG))
    rpbT = np.ascontiguousarray(g.transpose(0, 1, 3, 2, 4).reshape(2, 8, 64, 15 * 64)).astype(f32)
    common = dict(
        w_mod=np.ascontiguousarray(inputs["w_mod"], f32), b_mod=np.ascontiguousarray(inputs["b_mod"], f32),
        norm_pre=np.ascontiguousarray(inputs["norm_pre"], f32), norm_post=np.ascontiguousarray(inputs["norm_post"], f32),
        w_in_e=w_in_e, w_in_o=np.ascontiguousarray(inputs["w_in_odd"], f32),
        w_out=np.ascontiguousarray(inputs["w_out"], f32),
        lam_a=np.ascontiguousarray(np.asarray(inputs["lam_a"], f32).reshape(2, 128)),
        subln_a=np.ascontiguousarray(inputs["subln_a"], f32),
        conv_b=np.ascontiguousarray(np.asarray(inputs["conv_b"], f32).reshape(2, 3, 4, 128).transpose(0, 3, 2, 1).reshape(2, 128, 12)),
        rpbT=rpbT)
    common.update(host_constants())
    x = np.asarray(inputs["x"], f32)
    ctx = np.asarray(inputs["ctx"], f32)
    c = np.asarray(inputs["c"], f32)
    c_ctx = np.asarray(inputs["c_ctx"], f32)
    maps = []
    for b in range(8):
        m = dict(common)
        m["x_in"] = np.ascontiguousarray(np.concatenate([ctx[b], x[b]], axis=0))
        m["cvec"] = np.ascontiguousarray(np.stack([c_ctx, c[b]], axis=0).reshape(2, 8, 128).transpose(2, 0, 1).reshape(128, 16))
        maps.append(m)
    return maps


def kernel(**inputs):
    maps = host_layout(inputs)
    nc = build(DEPTH)
    res = run_bass_kernel_spmd(nc, maps, core_ids=list(range(8)))
    return np.stack([np.asarray(r["out"], np.float32) for r in res.results], axis=0)
```

```python
import math
from contextlib import ExitStack

import numpy as np
import ml_dtypes
import concourse.bass as bass
import concourse.mybir as mybir
from concourse.bass_utils import run_bass_kernel_spmd

F32 = mybir.dt.float32
BF16 = mybir.dt.bfloat16
AF = mybir.ActivationFunctionType
ALU = mybir.AluOpType
AX = mybir.AxisListType

D = 1024
S = 4096
L = 256
T = S + L
NT = T // 128
DEPTH = 4
EPS = 1e-6
GW = 64
ZC = T + 3
NEG = -30000.0

SAME_ENGINE_SYNC = True
KSTOP = ""
NDS = 40
NSW = 8


_UN = [0]


def un(name):
    _UN[0] += 1
    return "%s_%d" % (name, _UN[0])


class Ins:
    __slots__ = ("eng", "fn", "waits", "flag", "cnt", "idx", "dma")

    def __init__(self, eng, fn):
        self.eng = eng
        self.fn = fn
        self.waits = []
        self.flag = False
        self.cnt = None
        self.idx = None
        self.dma = None


class Sched:
    ENGS = ("pe", "act", "dve", "pool", "sp")

    def __init__(self, nc, stack):
        self.nc = nc
        self.sem = {e: stack.enter_context(nc.semaphore("s_" + e)) for e in self.ENGS}
        self.dsem = [stack.enter_context(nc.semaphore("d%d" % i)) for i in range(NDS + NSW)]
        self.dcnt = [0] * (NDS + NSW)
        self.dnext = 0
        self.dnext_sw = 0
        self.ins = {e: [] for e in self.ENGS}
        self.total = {e: 0 for e in self.ENGS}
        self.nidx = {e: 0 for e in self.ENGS}
        self.last_w = {}
        self.readers = {}
        self.seen = {e: {} for e in self.ENGS}
        self.rings = {}

    def ring(self, name, n):
        i = self.rings.get(name, 0)
        self.rings[name] = i + 1
        return i % n

    def _add_wait(self, rec, ev):
        eng = rec.eng
        if ev[0] == "eng":
            t = ev[1]
            if t.eng == eng and (eng == "pe" or eng == "sp" or not SAME_ENGINE_SYNC):
                return
            if self.seen[eng].get(t.eng, -1) >= t.idx:
                return
            self.seen[eng][t.eng] = t.idx
            t.flag = True
            rec.waits.append(("eng", t))
        else:
            _, j, val = ev
            key = ("d", j)
            if self.seen[eng].get(key, 0) >= val:
                return
            self.seen[eng][key] = val
            rec.waits.append(("dma", j, val))

    def _deps(self, rec, reads, writes):
        for k in reads:
            ev = self.last_w.get(k)
            if ev is not None:
                self._add_wait(rec, ev)
        for k in writes:
            ev = self.last_w.get(k)
            if ev is not None:
                self._add_wait(rec, ev)
            rd = self.readers.get(k)
            if rd:
                for ev2 in rd["eng"].values():
                    self._add_wait(rec, ev2)
                for ev2 in rd["dma"]:
                    self._add_wait(rec, ev2)

    def _record(self, ev, reads, writes):
        for k in reads:
            rd = self.readers.setdefault(k, {"eng": {}, "dma": []})
            if ev[0] == "eng":
                rd["eng"][ev[1].eng] = ev
            else:
                rd["dma"].append(ev)
        for k in writes:
            self.last_w[k] = ev
            self.readers[k] = {"eng": {}, "dma": []}

    def op(self, eng, fn, reads=(), writes=()):
        rec = Ins(eng, fn)
        rec.idx = self.nidx[eng]
        self.nidx[eng] += 1
        self._deps(rec, reads, writes)
        self.ins[eng].append(rec)
        self._record(("eng", rec), reads, writes)
        return rec

    def dma(self, out, in_, reads=(), writes=(), q="sp", slow=False):
        kw = {"allow_slow_non_contiguous": True} if slow else {}
        rec = Ins(q, lambda e: e.dma_start(out=out, in_=in_, **kw))
        rec.idx = self.nidx[q]
        self.nidx[q] += 1
        if q == "pool":
            j = NDS + self.dnext_sw % NSW
            self.dnext_sw += 1
        else:
            j = self.dnext % NDS
            self.dnext += 1
        if self.dcnt[j] > 0:
            self._add_wait(rec, ("dma", j, self.dcnt[j]))
        self._deps(rec, reads, writes)
        self.dcnt[j] += 16
        rec.dma = j
        self.ins[q].append(rec)
        self._record(("dma", j, self.dcnt[j]), reads, writes)
        return rec

    def mm(self, out, lhsT, rhs, start, stop, reads, writes, tp=None):
        kw = {}
        if tp is not None:
            kw["tile_position"] = tp
        return self.op("pe", lambda e: e.matmul(out, lhsT=lhsT, rhs=rhs, start=start, stop=stop,
                                                skip_group_check=True, **kw), reads, writes)

    def tr(self, out, in_, ident, reads, writes):
        return self.op("pe", lambda e: e.transpose(out, in_, ident), reads, writes)

    def act(self, out, in_, func, reads, writes, scale=None, bias=None, accum_out=None):
        kw = {}
        if scale is not None:
            kw["scale"] = scale
        if bias is not None:
            kw["bias"] = bias
        if accum_out is not None:
            kw["accum_out"] = accum_out
        return self.op("act", lambda e: e.activation(out=out, in_=in_, func=func, **kw), reads, writes)

    def tt(self, eng, out, in0, in1, op, reads, writes):
        return self.op(eng, lambda e: e.tensor_tensor(out=out, in0=in0, in1=in1, op=op), reads, writes)

    def ts(self, eng, out, in0, s1, s2, op0, op1, reads, writes):
        if op1 is None:
            return self.op(eng, lambda e: e.tensor_scalar(out=out, in0=in0, scalar1=s1, scalar2=None, op0=op0),
                           reads, writes)
        return self.op(eng, lambda e: e.tensor_scalar(out=out, in0=in0, scalar1=s1, scalar2=s2, op0=op0, op1=op1),
                       reads, writes)

    def stt(self, out, in0, scalar, in1, op0, op1, reads, writes):
        return self.op("dve", lambda e: e.scalar_tensor_tensor(out=out, in0=in0, scalar=scalar, in1=in1,
                                                               op0=op0, op1=op1), reads, writes)

    def cp(self, eng, out, in_, reads, writes):
        if eng == "act":
            return self.act(out, in_, AF.Copy, reads, writes)
        return self.op(eng, lambda e: e.tensor_copy(out=out, in_=in_), reads, writes)

    def red(self, out, in_, reads, writes, op=ALU.add):
        return self.op("dve", lambda e: e.tensor_reduce(out=out, in_=in_, axis=AX.X, op=op), reads, writes)

    def recip(self, out, in_, reads, writes):
        return self.op("dve", lambda e: e.reciprocal(out=out, in_=in_), reads, writes)

    def rsqrt(self, ap, key):
        self.act(ap, ap, AF.Ln, [key], [key])
        self.act(ap, ap, AF.Exp, [key], [key], scale=-0.5)

    def memset(self, eng, ap, val, writes):
        return self.op(eng, lambda e: e.memset(ap, val), (), writes)

    def flush(self):
        nc = self.nc
        for e in self.ENGS:
            for rec in reversed(self.ins[e]):
                if rec.dma is None:
                    rec.flag = True
                    break
        for e in self.ENGS:
            for rec in self.ins[e]:
                if rec.flag and rec.dma is None:
                    self.total[e] += 1
                    rec.cnt = self.total[e]
        lists = {e: self.ins[e] for e in self.ENGS}
        total = dict(self.total)
        dcnt = list(self.dcnt)
        sem = self.sem
        dsem = self.dsem

        def body(eng_name):
            def run(e):
                for rec in lists[eng_name]:
                    for w in rec.waits:
                        if w[0] == "eng":
                            e.wait_ge(sem[w[1].eng], w[1].cnt)
                        else:
                            e.wait_ge(dsem[w[1]], w[2])
                    ins = rec.fn(e)
                    if rec.dma is not None:
                        ins.then_inc(dsem[rec.dma], 16)
                    elif rec.flag:
                        ins.then_inc(sem[eng_name], 1)
                for o in self.ENGS:
                    if o != eng_name and o != "sp" and total[o] > 0:
                        e.wait_ge(sem[o], total[o])
                for j in range(NDS + NSW):
                    if dcnt[j] > 0:
                        e.wait_ge(dsem[j], dcnt[j])
            return run

        with nc.Block() as block:
            block.tensor(body("pe"))
            block.scalar(body("act"))
            block.vector(body("dve"))
            block.gpsimd(body("pool"))
            block.sync(body("sp"))
        self.ins = {e: [] for e in self.ENGS}
        self.last_w = {}
        self.readers = {}
        self.seen = {e: {} for e in self.ENGS}


def lambda_init(layer):
    return 0.8 - 0.6 * math.exp(-0.3 * layer)


def chunks_list():
    res = [(0, L)]
    for c in range(S // 512):
        res.append((L + 512 * c, 512))
    return res


def build(nl=DEPTH, dbg=False):
    nc = bass.Bass("TRN2", target_bir_lowering=False)
    dt = nc.dram_tensor

    def din(name, shape, dtype=F32):
        return dt(name, list(shape), dtype, kind="ExternalInput").ap()

    x_in = din("x_in", [T, D])
    cvec = din("cvec", [128, 16])
    w_mod = din("w_mod", [DEPTH, D, 3 * D])
    b_mod = din("b_mod", [DEPTH, 3 * D])
    norm_pre = din("norm_pre", [DEPTH, D])
    norm_post = din("norm_post", [DEPTH, D])
    w_in_e = din("w_in_e", [2, D, 5120])
    w_in_o = din("w_in_o", [2, D, 3072])
    w_out = din("w_out", [DEPTH, D, D])
    lam_a = din("lam_a", [2, 128])
    subln_a = din("subln_a", [2, 64])
    conv_b = din("conv_b", [2, 128, 12])
    rpbT = din("rpbT", [2, 8, 64, 960])
    ropec = din("ropec", [128, T])
    ropes = din("ropes", [128, T])
    identf = din("identf", [128, 128])
    dftc = din("dftc", [16, 128, 32 * 256], BF16)
    dfts = din("dfts", [16, 128, 32 * 256], BF16)
    dftc_l = din("dftc_l", [128, 2 * 256], BF16)
    dfts_l = din("dfts_l", [128, 2 * 256], BF16)
    cs64 = din("cs64", [128, 256], BF16)
    out = dt("out", [S, D], F32, kind="ExternalOutput").ap()
    xs = dt("xs", [T, D], F32, kind="ExternalOutput" if dbg else "Internal").ap()
    qT_d = dt("qT_d", [512, T], BF16).ap()
    kT_d = dt("kT_d", [512, T], BF16).ap()
    vx_d = dt("vx_d", [T, 520], BF16).ap()
    sg_d = dt("sg_d", [T, 512], F32).ap()
    zT_d = dt("zT_d", [512, ZC], F32).ap()
    wT_d = dt("wT_d", [512, ZC], F32).ap()
    z1_d = dt("z1_d", [T, 512], BF16).ap()
    z2_d = dt("z2_d", [T, 512], BF16).ap()
    yg_d = dt("yg_d", [T, 512], F32).ap()
    sg2_d = dt("sg2_d", [T, 512], F32).ap()

    stack = ExitStack()
    with stack:
        sc = Sched(nc, stack)
        sb = lambda name, shape, dtype=F32: stack.enter_context(nc.sbuf_tensor(un(name), list(shape), dtype))

        ident = sb("ident", [128, 128])
        identb = sb("identb", [128, 128], BF16)
        modA = [sb("modA%d" % i, [128, D]) for i in range(2)]
        modB = [sb("modB%d" % i, [128, D]) for i in range(2)]
        modG = [sb("modG%d" % i, [128, D]) for i in range(2)]
        scT = sb("scT", [128, 2, 8])
        scB = sb("scB", [128, 2, 8, 128])
        ps_all = stack.enter_context(nc.psum_tensor("ps_all", [128, 8, 512], F32))

        sc.dma(ident[:], identf[:, :], (), ["ident"])
        sc.cp("dve", identb[:], ident[:], ["ident"], ["identb"])
        sc.dma(scT[:].rearrange("p a k -> p (a k)"), cvec[:, :], (), ["scT"])
        sc.act(scT[:], scT[:], AF.Silu, ["scT"], ["scT"])
        sc.cp("dve", scB[:], scT[:].unsqueeze(3).to_broadcast([128, 2, 8, 128]), ["scT"], ["scB"])
        sc.flush()

        for l in range(nl):
            even = (l % 2 == 0)
            j = l // 2
            last = (l == DEPTH - 1)
            src_x = x_in if l == 0 else xs
            with ExitStack() as ph:
                psb = lambda name, shape, dtype=F32: ph.enter_context(nc.sbuf_tensor(un(name), list(shape), dtype))
                wst = [psb("wst%d" % i, [128, 8, 512]) for i in range(3)]
                bmb = psb("bmb", [128, 3 * D])
                npre = psb("npre", [128, D])
                npost = psb("npost", [128, D])
                sc.dma(bmb[:], b_mod[l, :].partition_broadcast(128), (), ["bmb"])
                sc.dma(npre[:], norm_pre[l, :].partition_broadcast(128), (), ["npre"])
                sc.dma(npost[:], norm_post[l, :].partition_broadcast(128), (), ["npost"])
                for cc in range(6):
                    wb = sc.ring("wst", 3)
                    sc.dma(wst[wb][:], w_mod[l, :, cc * 512:(cc + 1) * 512].rearrange("(k p) c -> p k c", p=128),
                           (), [("wst", wb)])
                    for who in range(2):
                        pb = sc.ring("psM", 4)
                        pt = ps_all[:, pb, :]
                        for kc in range(8):
                            sc.mm(pt, scB[:, who, kc, :], wst[wb][:, kc, :], kc == 0, kc == 7,
                                  ["scB", ("wst", wb)], [("ps", pb)])
                        part = cc // 2
                        cs = (cc % 2) * 512
                        bsl = bmb[:, cc * 512:(cc + 1) * 512]
                        if part == 0:
                            sc.tt("dve", modB[who][:, cs:cs + 512], pt, bsl, ALU.add,
                                  [("ps", pb), "bmb"], [("modB", who)])
                        elif part == 1:
                            sc.stt(modA[who][:, cs:cs + 512], pt, 1.0, bsl, ALU.add, ALU.add,
                                   [("ps", pb), "bmb"], [("modA", who)])
                            sc.tt("pool", modA[who][:, cs:cs + 512], modA[who][:, cs:cs + 512], npre[:, cs:cs + 512],
                                  ALU.mult, [("modA", who), "npre"], [("modA", who)])
                        else:
                            sc.tt("dve", modG[who][:, cs:cs + 512], pt, bsl, ALU.add,
                                  [("ps", pb), "bmb"], [("modG", who)])
                            sc.tt("pool", modG[who][:, cs:cs + 512], modG[who][:, cs:cs + 512], npost[:, cs:cs + 512],
                                  ALU.mult, [("modG", who), "npost"], [("modG", who)])
                sc.flush()

            if even:
                phaseA_even(nc, sc, l, j, src_x, w_in_e, ropec, ropes, ident, modA, modB, ps_all,
                            qT_d, kT_d, vx_d, sg_d, zT_d, wT_d)
                phaseB_even(nc, sc, l, j, last, src_x, xs, out, w_out, lam_a, subln_a, conv_b, ident, modG, ps_all,
                            qT_d, kT_d, vx_d, sg_d, zT_d, wT_d)
            else:
                phaseA_odd(nc, sc, l, j, src_x, w_in_o, cs64, ident, modA, modB, ps_all,
                           qT_d, kT_d, vx_d, sg_d, sg2_d, z1_d, z2_d)
                if KSTOP == "A":
                    break
                phaseB1_odd(nc, sc, l, last, dftc, dfts, dftc_l, dfts_l, ps_all, z1_d, z2_d, sg2_d, yg_d)
                if KSTOP == "B1":
                    break
                phaseB2_odd(nc, sc, l, j, last, src_x, xs, out, w_out, rpbT, ident, identb, modG, ps_all,
                            qT_d, kT_d, vx_d, sg_d, yg_d)
    return nc


def load_cast_weights(nc, sc, ph, Wb, w_ap, ncols, tag):
    CW = 256
    NB = 4
    wst = [ph.enter_context(nc.sbuf_tensor(un("%s_st%d" % (tag, i)), [128, 8, CW], F32)) for i in range(NB)]
    for cc in range(ncols // CW):
        b = sc.ring(tag + "st", NB)
        sc.dma(wst[b][:], w_ap[:, cc * CW:(cc + 1) * CW].rearrange("(k p) c -> p k c", p=128), (), [(tag + "st", b)])
        eng = "act" if cc % 2 == 0 else "dve"
        sc.cp(eng, Wb[:, :, cc * CW:(cc + 1) * CW], wst[b][:], [(tag + "st", b)], [tag])


def norm_steps(nc, sc, src_x, t0, ntok, who, xt, junk, ss, rstd, tmp, hx, hxT, hb, ident, modA, modB, ps_all):
    nt = ntok // 128
    hbuf = {}

    def pre(ti):
        xb = sc.ring("xt", 2)
        r0 = t0 + ti * 128
        sc.dma(xt[xb][:], src_x[r0:r0 + 128, :], [("xs", r0 // 128)], [("xt", xb)])
        sc.act(junk[:], xt[xb][:], AF.Square, [("xt", xb)], ["junk", "ss"], accum_out=ss[:, 0:1])
        sc.ts("dve", rstd[:], ss[:, 0:1], 1.0 / D, EPS, ALU.mult, ALU.add, ["ss"], ["rstd"])
        sc.rsqrt(rstd[:], "rstd")
        sc.stt(tmp[:], xt[xb][:], rstd[:, 0:1], modA[who][:], ALU.mult, ALU.mult,
               [("xt", xb), "rstd", ("modA", who)], ["tmp"])
        hi = sc.ring("hx", 2)
        hbuf[ti] = hi
        sc.tt("pool", hx[hi][:], tmp[:], modB[who][:], ALU.add, ["tmp", ("modB", who)], [("hx", hi)])

    def tr(ti):
        hi = hbuf[ti]
        for kc in range(8):
            bank = 6 + kc // 4
            sc.tr(ps_all[:, bank, (kc % 4) * 128:(kc % 4 + 1) * 128], hx[hi][:, kc * 128:(kc + 1) * 128], ident[:],
                  [("hx", hi), "ident"], [("ps", bank)])
        for half in range(2):
            sc.cp("act", hxT[hb][:, half * 4:(half + 1) * 4, ti * 128:(ti + 1) * 128],
                  ps_all[:, 6 + half, :].rearrange("p (k t) -> p k t", k=4),
                  [("ps", 6 + half)], [("hxT", hb)])

    order = []
    for ti in range(nt):
        order.append(("pre", ti))
        if ti >= 1:
            order.append(("tr", ti - 1))
    order.append(("tr", nt - 1))
    return [(lambda k=k, ti=ti: pre(ti) if k == "pre" else tr(ti)) for k, ti in order]


class Ticker:
    def __init__(self, steps, total):
        self.steps = list(steps)
        self.n = len(self.steps)
        self.total = max(total, 1)
        self.t = 0
        self.done = 0

    def tick(self):
        self.t += 1
        want = min(self.n, (self.t * self.n + self.total - 1) // self.total)
        while self.done < want:
            self.steps[self.done]()
            self.done += 1

    def flush(self):
        while self.done < self.n:
            self.steps[self.done]()
            self.done += 1


def phaseA_even(nc, sc, l, j, src_x, w_in_e, ropec, ropes, ident, modA, modB, ps_all,
                qT_d, kT_d, vx_d, sg_d, zT_d, wT_d):
    with ExitStack() as ph:
        psb = lambda name, shape, dtype=F32: ph.enter_context(nc.sbuf_tensor(un(name), list(shape), dtype))
        Wb = psb("Wb", [128, 8, 5120], BF16)
        with ExitStack() as ph2:
            load_cast_weights(nc, sc, ph2, Wb, w_in_e[j], 5120, "Wb")
            sc.flush()
        xt = [psb("xt%d" % i, [128, D]) for i in range(2)]
        junk = psb("junk", [128, D])
        tmp = psb("tmp", [128, D])
        hx = [psb("hx%d" % i, [128, D]) for i in range(2)]
        ss = psb("ss", [128, 1])
        rstd = psb("rstd", [128, 1])
        hxT = [psb("hxT%d" % i, [128, 8, 512], BF16) for i in range(2)]
        rc = [psb("rc%d" % i, [128, 512]) for i in range(2)]
        rs_ = [psb("rs%d" % i, [128, 512]) for i in range(2)]
        t1 = [psb("t1_%d" % i, [128, 512]) for i in range(2)]
        t2 = [psb("t2_%d" % i, [128, 512]) for i in range(2)]
        qo = [psb("qo%d" % i, [128, 512], BF16) for i in range(3)]
        fo = [psb("fo%d" % i, [128, 512]) for i in range(3)]
        vxt = [psb("vxt%d" % i, [128, 8, 65], BF16) for i in range(2)]
        zpad = psb("zpad", [128, 4])
        for i in range(2):
            sc.memset("pool", vxt[i][:], 1.0, [("vxt", i)])
        sc.memset("pool", zpad[:], 0.0, ["zpad"])
        for dst in (zT_d, wT_d):
            for i4 in range(4):
                for col in (0, L + 1, ZC - 1):
                    sc.dma(dst[i4 * 128:(i4 + 1) * 128, col:col + 1], zpad[:, 0:1], ["zpad"], [], slow=True)

        chl = chunks_list()

        def mk_steps(ci, hb_):
            t0_, ntok_ = chl[ci]
            return norm_steps(nc, sc, src_x, t0_, ntok_, 0 if ci == 0 else 1, xt, junk, ss, rstd, tmp, hx, hxT, hb_,
                              ident, modA, modB, ps_all)

        hb_next = sc.ring("hxT", 2)
        for st in mk_steps(0, hb_next):
            st()
        for ci, (t0, ntok) in enumerate(chl):
            who = 0 if ci == 0 else 1
            hb = hb_next
            if ci + 1 < len(chl):
                hb_next = sc.ring("hxT", 2)
                tk = Ticker(mk_steps(ci + 1, hb_next), 40)
            else:
                tk = Ticker([], 1)
            rb = sc.ring("rope", 2)
            sc.dma(rc[rb][:, :ntok], ropec[:, t0:t0 + ntok], (), [("rc", rb)])
            sc.dma(rs_[rb][:, :ntok], ropes[:, t0:t0 + ntok], (), [("rs", rb)])

            def fm(col0):
                tk.tick()
                pb = sc.ring("psA", 6)
                for kc in range(8):
                    sc.mm(ps_all[:, pb, :ntok], Wb[:, kc, col0:col0 + 128], hxT[hb][:, kc, :ntok], kc == 0, kc == 7,
                          ["Wb", ("hxT", hb)], [("ps", pb)])
                return pb

            zc0 = t0 + 1 if ci == 0 else t0 + 2
            for which, dst in ((0, qT_d), (1, kT_d)):
                for i4 in range(4):
                    p1 = fm(which * 1024 + i4 * 128)
                    p2 = fm(which * 1024 + 512 + i4 * 128)
                    tb = sc.ring("t12", 2)
                    sc.tt("dve", t1[tb][:, :ntok], ps_all[:, p1, :ntok], rc[rb][:, :ntok], ALU.mult,
                          [("ps", p1), ("rc", rb)], [("t1", tb)])
                    sc.tt("dve", t2[tb][:, :ntok], ps_all[:, p2, :ntok], rs_[rb][:, :ntok], ALU.mult,
                          [("ps", p2), ("rs", rb)], [("t2", tb)])
                    ob = sc.ring("qo", 3)
                    sc.tt("pool", qo[ob][:, :ntok], t1[tb][:, :ntok], t2[tb][:, :ntok], ALU.add,
                          [("t1", tb), ("t2", tb)], [("qo", ob)])
                    sc.dma(dst[i4 * 128:(i4 + 1) * 128, t0:t0 + ntok], qo[ob][:, :ntok], [("qo", ob)], [])
            for i4 in range(4):
                pc_ = fm(3584 + i4 * 128)
                pu = fm(4096 + i4 * 128)
                tb = sc.ring("t12", 2)
                sc.cp("act", t1[tb][:, :ntok], ps_all[:, pc_, :ntok], [("ps", pc_)], [("t1", tb)])
                ob = sc.ring("fo", 3)
                sc.tt("dve", fo[ob][:, :ntok], ps_all[:, pu, :ntok], t1[tb][:, :ntok], ALU.mult,
                      [("ps", pu), ("t1", tb)], [("fo", ob)])
                sc.dma(zT_d[i4 * 128:(i4 + 1) * 128, zc0:zc0 + ntok], fo[ob][:, :ntok], [("fo", ob)], [])
                pbb = fm(3072 + i4 * 128)
                pg = fm(4608 + i4 * 128)
                sc.act(t2[tb][:, :ntok], ps_all[:, pg, :ntok], AF.Silu, [("ps", pg)], [("t2", tb)])
                ob = sc.ring("fo", 3)
                sc.tt("dve", fo[ob][:, :ntok], ps_all[:, pbb, :ntok], t2[tb][:, :ntok], ALU.mult,
                      [("ps", pbb), ("t2", tb)], [("fo", ob)])
                sc.dma(wT_d[i4 * 128:(i4 + 1) * 128, zc0:zc0 + ntok], fo[ob][:, :ntok], [("fo", ob)], [])
            for ti in range(ntok // 128):
                r0 = t0 + ti * 128
                pb = sc.ring("psA", 6)
                for kc in range(8):
                    sc.mm(ps_all[:, pb, :], hxT[hb][:, kc, ti * 128:(ti + 1) * 128], Wb[:, kc, 2048:2560],
                          kc == 0, kc == 7, ["Wb", ("hxT", hb)], [("ps", pb)])
                vb = sc.ring("vxt", 2)
                sc.cp("act", vxt[vb][:, :, 0:64], ps_all[:, pb, :].rearrange("p (h d) -> p h d", h=8),
                      [("ps", pb)], [("vxt", vb)])
                sc.dma(vx_d[r0:r0 + 128, :], vxt[vb][:].rearrange("p h d -> p (h d)"), [("vxt", vb)], [])
                pb = sc.ring("psA", 6)
                for kc in range(8):
                    sc.mm(ps_all[:, pb, :], hxT[hb][:, kc, ti * 128:(ti + 1) * 128], Wb[:, kc, 2560:3072],
                          kc == 0, kc == 7, ["Wb", ("hxT", hb)], [("ps", pb)])
                ob = sc.ring("fo", 3)
                sc.act(fo[ob][:], ps_all[:, pb, :], AF.Silu, [("ps", pb)], [("fo", ob)])
                sc.dma(sg_d[r0:r0 + 128, :], fo[ob][:], [("fo", ob)], [])
            tk.flush()
        sc.flush()


def out_proj_tile(nc, sc, l, last, src_x, xs, out, r0, mxT, tcol, Wo, modG, who, xt, junk, ss2, rstd, tmp, ps_all,
                  banks=(6, 7)):
    xb = sc.ring("xt", 2)
    sc.dma(xt[xb][:], src_x[r0:r0 + 128, :], [("xs", r0 // 128)], [("xt", xb)])
    for half in range(2):
        bank = banks[half]
        for kc in range(8):
            sc.mm(ps_all[:, bank, :], mxT[:, kc, tcol:tcol + 128], Wo[:, kc, half * 512:(half + 1) * 512],
                  kc == 0, kc == 7, ["mxT", "Wo"], [("ps", bank)])
        sc.act(junk[:, 0:512], ps_all[:, bank, :], AF.Square, [("ps", bank)], ["junk", ("ss2", half)],
               accum_out=ss2[:, half:half + 1])
    sc.tt("dve", rstd[:], ss2[:, 0:1], ss2[:, 1:2], ALU.add, [("ss2", 0), ("ss2", 1)], ["rstd"])
    sc.ts("dve", rstd[:], rstd[:], 1.0 / D, EPS, ALU.mult, ALU.add, ["rstd"], ["rstd"])
    sc.rsqrt(rstd[:], "rstd")
    for half in range(2):
        bank = banks[half]
        sc.stt(tmp[:, half * 512:(half + 1) * 512], ps_all[:, bank, :], rstd[:, 0:1],
               modG[who][:, half * 512:(half + 1) * 512], ALU.mult, ALU.mult,
               [("ps", bank), "rstd", ("modG", who)], [("tmp", half)])
    sc.tt("pool", xt[xb][:], xt[xb][:], tmp[:], ALU.add, [("xt", xb), ("tmp", 0), ("tmp", 1)], [("xt", xb)])
    if last:
        sc.dma(out[r0 - L:r0 - L + 128, :], xt[xb][:], [("xt", xb)], [], q="pool")
    else:
        sc.dma(xs[r0:r0 + 128, :], xt[xb][:], [("xt", xb)], [("xs", r0 // 128)], q="pool")


def phaseB_even(nc, sc, l, j, last, src_x, xs, out, w_out, lam_a, subln_a, conv_b, ident, modG, ps_all,
                qT_d, kT_d, vx_d, sg_d, zT_d, wT_d):
    scale = 32 ** -0.5
    li = lambda_init(l)
    with ExitStack() as ph:
        psb = lambda name, shape, dtype=F32: ph.enter_context(nc.sbuf_tensor(un(name), list(shape), dtype))
        Wo = psb("Wo", [128, 8, D], BF16)
        kT = psb("kT", [128, 4, T], BF16)
        V = psb("V", [128, NT, 520], BF16)
        lam_t = psb("lam_t", [128, 128])
        lam_p = psb("lam_p", [128, 64])
        lam_s = psb("lam_s", [128, 4])
        neglam = psb("neglam", [128, 1])
        subl = psb("subl", [128, 8, 64])
        cw = psb("cw", [128, 4, 3])
        with ExitStack() as ph2:
            load_cast_weights(nc, sc, ph2, Wo, w_out[l], D, "Wo")
            for i4 in range(4):
                sc.dma(kT[:, i4, :], kT_d[i4 * 128:(i4 + 1) * 128, :], (), ["kT"])
            for g in range(2):
                sc.dma(V[:, g * 17:(g + 1) * 17, :],
                       vx_d[g * 17 * 128:(g + 1) * 17 * 128, :].rearrange("(t p) c -> p t c", p=128), (), ["V"])
            sc.dma(lam_t[:], lam_a[j, :].partition_broadcast(128), (), ["lam_t"])
            lv = lam_t[:].rearrange("p (a b c) -> p a b c", a=2, b=2)
            sc.tt("dve", lam_p[:].rearrange("p (a c) -> p a c", a=2), lv[:, :, 0, :], lv[:, :, 1, :], ALU.mult,
                  ["lam_t"], ["lam_p"])
            sc.red(lam_s[:, 0:2], lam_p[:].rearrange("p (a c) -> p a c", a=2), ["lam_p"], ["lam_s"])
            sc.act(lam_s[:, 2:4], lam_s[:, 0:2], AF.Exp, ["lam_s"], ["lam_s"])
            sc.tt("dve", neglam[:], lam_s[:, 3:4], lam_s[:, 2:3], ALU.subtract, ["lam_s"], ["neglam"])
            sc.ts("dve", neglam[:], neglam[:], -li, None, ALU.add, None, ["neglam"], ["neglam"])
            for h in range(8):
                sc.dma(subl[:, h, :], subln_a[j, :].partition_broadcast(128), (), ["subl"])
            sc.ts("dve", subl[:], subl[:], 1.0 - li, None, ALU.mult, None, ["subl"], ["subl"])
            sc.dma(cw[:].rearrange("p a k -> p (a k)"), conv_b[j, :, :], (), ["cw"])
            sc.flush()
        qT = [psb("qT%d" % i, [128, 4, 512], BF16) for i in range(2)]
        P = [psb("P%d" % i, [128, 2, 512], BF16) for i in range(4)]
        osb = psb("osb", [65, 4, 512])
        O = [psb("O%d" % i, [128, 4, 2, 65]) for i in range(2)]
        R = psb("R", [128, 16])
        Dd = psb("Dd", [128, 4, 2, 64])
        D1 = psb("D1", [128, 4, 2, 64])
        sq = psb("sq", [128, 4, 2, 64])
        ssh = psb("ssh", [128, 4, 2])
        sgc = psb("sgc", [128, 4, 512])
        Yc = psb("Yc", [128, 4, 512])
        mxT = psb("mxT", [128, 8, 512], BF16)
        zc = [psb("zc%d" % i, [128, 514]) for i in range(2)]
        wc = [psb("wc%d" % i, [128, 512]) for i in range(2)]
        yc = psb("yc", [128, 512])
        xt = [psb("xt%d" % i, [128, D]) for i in range(2)]
        junk = psb("junk", [128, 512])
        tmp = psb("tmp", [128, D])
        ss2 = psb("ss2", [128, 2])
        rstd = psb("rstd", [128, 1])

        chl_b = chunks_list()
        qpre = {}
        for ci, (t0, nq) in enumerate(chl_b):
            if ci == 0 and last:
                continue
            who = 0 if ci == 0 else 1
            kts = [0, 1] if ci == 0 else list(range(NT))
            nqt = nq // 128
            if ci in qpre:
                qb = qpre.pop(ci)
            else:
                qb = sc.ring("qT", 2)
                for i4 in range(4):
                    sc.dma(qT[qb][:, i4, :nq], qT_d[i4 * 128:(i4 + 1) * 128, t0:t0 + nq], (), [("qT", qb)])
            for qt in range(nqt):
                sc.dma(sgc[:, qt, :], sg_d[t0 + qt * 128:t0 + (qt + 1) * 128, :], (), ["sgc"])
            if ci + 1 < len(chl_b):
                t0n, nqn = chl_b[ci + 1]
                qbn = sc.ring("qT", 2)
                for i4 in range(4):
                    sc.dma(qT[qbn][:, i4, :nqn], qT_d[i4 * 128:(i4 + 1) * 128, t0n:t0n + nqn], (), [("qT", qbn)])
                qpre[ci + 1] = qbn
            zc0 = t0 + 1 if ci == 0 else t0 + 2
            for i4 in range(4):
                zb = sc.ring("zc", 2)
                sc.dma(zc[zb][:, :nq + 2], zT_d[i4 * 128:(i4 + 1) * 128, zc0 - 1:zc0 + nq + 1], (), [("zc", zb)])
                sc.dma(wc[zb][:, :nq], wT_d[i4 * 128:(i4 + 1) * 128, zc0:zc0 + nq], (), [("wc", zb)])
                sc.ts("dve", yc[:, :nq], zc[zb][:, 0:nq], cw[:, i4, 0:1], None, ALU.mult, None,
                      [("zc", zb), "cw"], ["yc"])
                sc.stt(yc[:, :nq], zc[zb][:, 1:nq + 1], cw[:, i4, 1:2], yc[:, :nq], ALU.mult, ALU.add,
                       [("zc", zb), "cw", "yc"], ["yc"])
                sc.stt(yc[:, :nq], zc[zb][:, 2:nq + 2], cw[:, i4, 2:3], yc[:, :nq], ALU.mult, ALU.add,
                       [("zc", zb), "cw", "yc"], ["yc"])
                sc.tt("pool", mxT[:, 4 + i4, :nq], yc[:, :nq], wc[zb][:, :nq], ALU.mult,
                      ["yc", ("wc", zb)], ["mxT"])
            its = [(pr, ki, kt) for pr in range(4) for ki, kt in enumerate(kts)]

            def emit_S(n, half):
                pr, ki, kt = its[n]
                for m in range(2):
                    p0 = 64 * half + 32 * m
                    bank = 2 * half + m
                    sc.mm(ps_all[:, bank, :nq], kT[p0:p0 + 32, pr, kt * 128:(kt + 1) * 128],
                          qT[qb][p0:p0 + 32, pr, :nq], True, True, ["kT", ("qT", qb)], [("ps", bank)],
                          tp=(p0, 0))

            def emit_PV(n, half, pbuf):
                pr, ki, kt = its[n]
                h = 2 * pr + half
                for m in range(2):
                    bank = 4 + 2 * half + m
                    sc.mm(ps_all[0:65, bank, :nq], V[:, kt, h * 65:(h + 1) * 65], P[pbuf][:, m, :nq],
                          ki == 0, ki == len(kts) - 1, ["V", ("P", pbuf)], [("ps", bank)])

            emit_S(0, 0)
            emit_S(0, 1)
            pending = []
            for n, (pr, ki, kt) in enumerate(its):
                if pending and n >= pending[0][0]:
                    pending.pop(0)[1]()
                pbs = []
                for half in range(2):
                    pbuf = sc.ring("P", 4)
                    pbs.append(pbuf)
                    sc.act(P[pbuf][:, :, :nq], ps_all[:, 2 * half:2 * half + 2, :nq], AF.Exp,
                           [("ps", 2 * half), ("ps", 2 * half + 1)], [("P", pbuf)], scale=scale)
                if n + 1 < len(its):
                    emit_S(n + 1, 0)
                    emit_S(n + 1, 1)
                emit_PV(n, 0, pbs[0])
                emit_PV(n, 1, pbs[1])
                if ki != len(kts) - 1:
                    continue
                sc.cp("dve", osb[:, :, :nq], ps_all[0:65, 4:8, :nq], [("ps", 4), ("ps", 5), ("ps", 6), ("ps", 7)],
                      ["osb"])
                for jj in range(4):
                    half, m = jj // 2, jj % 2
                    h = 2 * pr + half
                    for qt in range(nqt):
                        sc.tr(ps_all[:, 4 + jj, qt * 65:(qt + 1) * 65], osb[:, jj, qt * 128:(qt + 1) * 128],
                              ident[0:65, 0:65], ["osb", "ident"], [("ps", 4 + jj)])
                    sc.cp("dve", O[m][:, 0:nqt, half, :],
                          ps_all[:, 4 + jj, 0:nqt * 65].rearrange("p (q d) -> p q d", d=65),
                          [("ps", 4 + jj)], [("O", m)])
                while pending:
                    pending.pop(0)[1]()
                Q = slice(0, nqt)
                R4 = R[:, 0:2 * nqt].rearrange("p (q h) -> p q h", h=2)
                R4b = R[:, 8:8 + 2 * nqt].rearrange("p (q h) -> p q h", h=2)
                sc.recip(R4, O[0][:, Q, :, 64], [("O", 0)], ["R"])
                sc.recip(R4b, O[1][:, Q, :, 64], [("O", 1)], ["R"])
                sc.ts("dve", R4b, R4b, neglam[:, 0:1], None, ALU.mult, None, ["R", "neglam"], ["R"])
                sc.tt("dve", Dd[:, Q], O[0][:, Q, :, 0:64], R4.unsqueeze(3).to_broadcast([128, nqt, 2, 64]),
                      ALU.mult, [("O", 0), "R"], ["Dd"])
                sc.tt("pool", D1[:, Q], O[1][:, Q, :, 0:64], R4b.unsqueeze(3).to_broadcast([128, nqt, 2, 64]),
                      ALU.mult, [("O", 1), "R"], ["D1"])
                sc.tt("dve", Dd[:, Q], Dd[:, Q], D1[:, Q], ALU.add, ["Dd", "D1"], ["Dd"])
                sc.tt("pool", sq[:, Q], Dd[:, Q], Dd[:, Q], ALU.mult, ["Dd"], ["sq"])
                sc.red(ssh[:, Q, :], sq[:, Q], ["sq"], ["ssh"])
                sc.ts("dve", ssh[:, Q, :], ssh[:, Q, :], 1.0 / 64, EPS, ALU.mult, ALU.add, ["ssh"], ["ssh"])

                def stage2(pr=pr):
                    sc.rsqrt(ssh[:, Q, :], "ssh")
                    sc.tt("dve", Dd[:, Q], Dd[:, Q], ssh[:, Q, :].unsqueeze(3).to_broadcast([128, nqt, 2, 64]),
                          ALU.mult, ["Dd", "ssh"], ["Dd"])
                    sc.tt("pool", Dd[:, Q], Dd[:, Q], subl[:, 0:2, :].unsqueeze(1).to_broadcast([128, nqt, 2, 64]),
                          ALU.mult, ["Dd", "subl"], ["Dd"])
                    sc.tt("pool", Yc[:, Q, pr * 128:(pr + 1) * 128], Dd[:, Q].rearrange("p q h d -> p q (h d)"),
                          sgc[:, Q, pr * 128:(pr + 1) * 128], ALU.mult, ["Dd", "sgc"], ["Yc"])

                pending.append((n + 12, stage2))
            while pending:
                pending.pop(0)[1]()
            for qt in range(nqt):
                for i4 in range(4):
                    sc.tr(ps_all[:, 6, i4 * 128:(i4 + 1) * 128], Yc[:, qt, i4 * 128:(i4 + 1) * 128], ident[:],
                          ["Yc", "ident"], [("ps", 6)])
                sc.cp("act", mxT[:, 0:4, qt * 128:(qt + 1) * 128],
                      ps_all[:, 6, :].rearrange("p (k t) -> p k t", k=4), [("ps", 6)], ["mxT"])
            for qt in range(nqt):
                out_proj_tile(nc, sc, l, last, src_x, xs, out, t0 + qt * 128, mxT, qt * 128, Wo, modG, who,
                              xt, junk, ss2, rstd, tmp, ps_all, banks=(4, 5) if qt % 2 == 0 else (6, 7))
        sc.flush()


def phaseA_odd(nc, sc, l, j, src_x, w_in_o, cs64, ident, modA, modB, ps_all,
               qT_d, kT_d, vx_d, sg_d, sg2_d, z1_d, z2_d):
    with ExitStack() as ph:
        psb = lambda name, shape, dtype=F32: ph.enter_context(nc.sbuf_tensor(un(name), list(shape), dtype))
        Wb = psb("Wbo", [128, 8, 3072], BF16)
        csb = psb("csb", [128, 256], BF16)
        with ExitStack() as ph2:
            load_cast_weights(nc, sc, ph2, Wb, w_in_o[j], 3072, "Wb")
            sc.dma(csb[:], cs64[:, :], (), ["csb"])
            sc.flush()
        xt = [psb("xt%d" % i, [128, D]) for i in range(2)]
        junk = psb("junk", [128, D])
        tmp = psb("tmp", [128, D])
        hx = [psb("hx%d" % i, [128, D]) for i in range(2)]
        ss = psb("ss", [128, 1])
        rstd = psb("rstd", [128, 1])
        hxT = [psb("hxT%d" % i, [128, 8, 512], BF16) for i in range(2)]
        qo = [psb("qo%d" % i, [128, 512], BF16) for i in range(3)]
        fo = [psb("fo%d" % i, [128, 512]) for i in range(3)]
        zo = [psb("zo%d" % i, [128, 512], BF16) for i in range(3)]
        uT = psb("uT", [128, 4, 512], BF16)
        vxt = [psb("vxt%d" % i, [128, 8, 65], BF16) for i in range(2)]
        for i in range(2):
            sc.memset("pool", vxt[i][:], 1.0, [("vxt", i)])

        chl = chunks_list()

        def mk_steps(ci, hb_):
            t0_, ntok_ = chl[ci]
            return norm_steps(nc, sc, src_x, t0_, ntok_, 0 if ci == 0 else 1, xt, junk, ss, rstd, tmp, hx, hxT, hb_,
                              ident, modA, modB, ps_all)

        hb_next = sc.ring("hxT", 2)
        for st in mk_steps(0, hb_next):
            st()
        for ci, (t0, ntok) in enumerate(chl):
            who = 0 if ci == 0 else 1
            hb = hb_next
            if ci + 1 < len(chl):
                hb_next = sc.ring("hxT", 2)
                tk = Ticker(mk_steps(ci + 1, hb_next), 24)
            else:
                tk = Ticker([], 1)

            def fm(col0):
                tk.tick()
                pb = sc.ring("psA", 6)
                for kc in range(8):
                    sc.mm(ps_all[:, pb, :ntok], Wb[:, kc, col0:col0 + 128], hxT[hb][:, kc, :ntok], kc == 0, kc == 7,
                          ["Wb", ("hxT", hb)], [("ps", pb)])
                return pb

            for which, dst in ((0, qT_d), (1, kT_d)):
                for i4 in range(4):
                    p1 = fm(which * 512 + i4 * 128)
                    ob = sc.ring("qo", 3)
                    sc.cp("act" if i4 % 2 == 0 else "dve", qo[ob][:, :ntok], ps_all[:, p1, :ntok],
                          [("ps", p1)], [("qo", ob)])
                    sc.dma(dst[i4 * 128:(i4 + 1) * 128, t0:t0 + ntok], qo[ob][:, :ntok], [("qo", ob)], [])
            for i4 in range(4):
                p1 = fm(2048 + i4 * 128)
                sc.cp("act" if i4 % 2 == 0 else "dve", uT[:, i4, :ntok], ps_all[:, p1, :ntok], [("ps", p1)], ["uT"])

            def tm(c0):
                tk.tick()
                pb = sc.ring("psA", 6)
                for kc in range(8):
                    sc.mm(ps_all[:, pb, :], hxT[hb][:, kc, ti * 128:(ti + 1) * 128], Wb[:, kc, c0:c0 + 512],
                          kc == 0, kc == 7, ["Wb", ("hxT", hb)], [("ps", pb)])
                return pb

            for ti in range(ntok // 128):
                r0 = t0 + ti * 128
                pb = tm(1024)
                vb = sc.ring("vxt", 2)
                sc.cp("act", vxt[vb][:, :, 0:64], ps_all[:, pb, :].rearrange("p (h d) -> p h d", h=8),
                      [("ps", pb)], [("vxt", vb)])
                sc.dma(vx_d[r0:r0 + 128, :], vxt[vb][:].rearrange("p h d -> p (h d)"), [("vxt", vb)], [])
                for c0, dst in ((1536, sg_d), (2560, sg2_d)):
                    pb = tm(c0)
                    ob = sc.ring("fo", 3)
                    sc.act(fo[ob][:], ps_all[:, pb, :], AF.Silu, [("ps", pb)], [("fo", ob)])
                    sc.dma(dst[r0:r0 + 128, :], fo[ob][:], [("fo", ob)], [])
                for which, dst in ((0, z1_d), (1, z2_d)):
                    pb = sc.ring("psA", 6)
                    for i4 in range(4):
                        sc.mm(ps_all[:, pb, i4 * 128:(i4 + 1) * 128], uT[:, i4, ti * 128:(ti + 1) * 128],
                              csb[:, which * 128:(which + 1) * 128], True, True, ["uT", "csb"], [("ps", pb)])
                    ob = sc.ring("zo", 3)
                    sc.cp("dve", zo[ob][:], ps_all[:, pb, :], [("ps", pb)], [("zo", ob)])
                    sc.dma(dst[r0:r0 + 128, :], zo[ob][:], [("zo", ob)], [])
            tk.flush()
        sc.flush()


def phaseB1_odd(nc, sc, l, last, dftc, dfts, dftc_l, dfts_l, ps_all, z1_d, z2_d, sg2_d, yg_d):
    with ExitStack() as ph:
        psb = lambda name, shape, dtype=F32: ph.enter_context(nc.sbuf_tensor(un(name), list(shape), dtype))
        Z = [psb("Z%d" % i, [128, 32, 512], BF16) for i in range(2)]
        Zc = [psb("Zc%d" % i, [128, 2, 512], BF16) for i in range(2)]
        Cc = [psb("Cc%d" % i, [128, 32 * 256], BF16) for i in range(2)]
        Sc = [psb("Sc%d" % i, [128, 32 * 256], BF16) for i in range(2)]
        Cl = psb("Cl", [128, 512], BF16)
        Sl = psb("Sl", [128, 512], BF16)
        sgd = [psb("sgd%d" % i, [128, 512]) for i in range(2)]
        yo = [psb("yo%d" % i, [128, 512]) for i in range(2)]
        for i, zd in enumerate((z1_d, z2_d)):
            for g in range(4):
                sc.dma(Z[i][:, g * 8:(g + 1) * 8, :],
                       zd[L + g * 1024:L + (g + 1) * 1024, :].rearrange("(t p) c -> p t c", p=128), (), [("Z", i)])
            sc.dma(Zc[i][:], zd[0:L, :].rearrange("(t p) c -> p t c", p=128), (), [("Zc", i)])
        sc.dma(Cl[:], dftc_l[:, :], (), ["Cl"])
        sc.dma(Sl[:], dfts_l[:, :], (), ["Sl"])

        def finish(pb, r0):
            gb = sc.ring("sgd", 2)
            sc.dma(sgd[gb][:], sg2_d[r0:r0 + 128, :], (), [("sgd", gb)])
            sc.tt("dve", yo[gb][:], ps_all[:, pb, :], sgd[gb][:], ALU.mult, [("ps", pb), ("sgd", gb)], [("yo", gb)])
            sc.dma(yg_d[r0:r0 + 128, :], yo[gb][:], [("yo", gb)], [], q="pool")

        if not last:
            for tt_ in range(2):
                pb = sc.ring("psF", 4)
                n = 0
                for (Mt, zi) in ((Cl, 0), (Sl, 1)):
                    for nt in range(2):
                        sc.mm(ps_all[:, pb, :], Mt[:, nt * 256 + tt_ * 128:nt * 256 + (tt_ + 1) * 128], Zc[zi][:, nt, :],
                              n == 0, n == 3, ["Cl", "Sl", ("Zc", zi)], [("ps", pb)])
                        n += 1
                finish(pb, tt_ * 128)
        for ch in range(16):
            cb = sc.ring("dft", 2)
            sc.dma(Cc[cb][:], dftc[ch, :, :], (), [("Cc", cb)])
            sc.dma(Sc[cb][:], dfts[ch, :, :], (), [("Sc", cb)])
            for tt_ in range(2):
                pb = sc.ring("psF", 4)
                n = 0
                for (Mt, key, zi) in ((Cc[cb], ("Cc", cb), 0), (Sc[cb], ("Sc", cb), 1)):
                    for nt in range(32):
                        sc.mm(ps_all[:, pb, :], Mt[:, nt * 256 + tt_ * 128:nt * 256 + (tt_ + 1) * 128], Z[zi][:, nt, :],
                              n == 0, n == 63, [key, ("Z", zi)], [("ps", pb)])
                        n += 1
                finish(pb, L + ch * 256 + tt_ * 128)
        sc.flush()


def phaseB2_odd(nc, sc, l, j, last, src_x, xs, out, w_out, rpbT, ident, identb, modG, ps_all,
                qT_d, kT_d, vx_d, sg_d, yg_d):
    scale = 0.125
    with ExitStack() as ph:
        psb = lambda name, shape, dtype=F32: ph.enter_context(nc.sbuf_tensor(un(name), list(shape), dtype))
        Wo = psb("Wo", [128, 8, D], BF16)
        Tb = psb("Tb", [128, 8, 960], BF16)
        kTc = psb("kTc", [128, 4, L], BF16)
        Vc = psb("Vc", [128, 2, 520], BF16)
        with ExitStack() as ph2:
            load_cast_weights(nc, sc, ph2, Wo, w_out[l], D, "Wo")
            tst = [ph2.enter_context(nc.sbuf_tensor(un("tst%d" % i), [128, 896], F32)) for i in range(2)]
            for h in range(8):
                tb = sc.ring("tst", 2)
                sc.dma(tst[tb][0:64, :], rpbT[j, h, :, 0:896], (), [("tst", tb)])
                sc.dma(tst[tb][64:128, :], rpbT[j, h, :, 64:960], (), [("tst", tb)])
                sc.act(Tb[:, h, 0:896], tst[tb][:], AF.Exp, [("tst", tb)], ["Tb"])
            for i4 in range(4):
                sc.dma(kTc[:, i4, :], kT_d[i4 * 128:(i4 + 1) * 128, 0:L], (), ["kTc"])
            sc.dma(Vc[:], vx_d[0:L, :].rearrange("(t p) c -> p t c", p=128), (), ["Vc"])
            sc.flush()
        kTw = [psb("kTw%d" % i, [128, 4, 1024], BF16) for i in range(2)]
        Vw0 = [psb("Vw0_%d" % i, [128, 8, 520], BF16) for i in range(2)]
        Vw1 = [psb("Vw1_%d" % i, [128, 7, 520], BF16) for i in range(2)]
        qT = [psb("qT%d" % i, [128, 4, 512], BF16) for i in range(2)]
        P = [psb("P%d" % i, [128, 2, 384], BF16) for i in range(3)]
        R = psb("R", [64, 8])
        Yo = psb("Yo", [64, 512])
        Y = [psb("Y%d" % i, [64, 512]) for i in range(2)]
        sgr = [psb("sgr%d" % i, [64, 512]) for i in range(2)]
        yg = [psb("yg%d" % i, [128, 512]) for i in range(2)]
        mxT = psb("mxT", [128, 8, 512], BF16)
        xt = [psb("xt%d" % i, [128, D]) for i in range(2)]
        junk = psb("junk", [128, 512])
        tmp = psb("tmp", [128, D])
        ss2 = psb("ss2", [128, 2])
        rstd = psb("rstd", [128, 1])
        obanks = (4, 5)

        for ci, (t0, nq) in enumerate(chunks_list()):
            if ci == 0 and last:
                continue
            who = 0 if ci == 0 else 1
            window = ci > 0
            nrow = nq // 64
            qb = sc.ring("qT", 2)
            for i4 in range(4):
                sc.dma(qT[qb][:, i4, :nq], qT_d[i4 * 128:(i4 + 1) * 128, t0:t0 + nq], (), [("qT", qb)])
            wb = 0
            wlo = 0
            if window:
                c = ci - 1
                lo = max(8 * c - 4, 0)
                wlo = min(lo, 48)
                wb = sc.ring("win", 2)
                for i4 in range(4):
                    sc.dma(kTw[wb][:, i4, :], kT_d[i4 * 128:(i4 + 1) * 128, L + wlo * 64:L + wlo * 64 + 1024], (),
                           [("kTw", wb)])
                sc.dma(Vw0[wb][:], vx_d[L + wlo * 64:L + wlo * 64 + 1024, :].rearrange("(t p) c -> p t c", p=128), (),
                       [("Vw", wb)])
                sc.dma(Vw1[wb][:], vx_d[L + (wlo + 1) * 64:L + (wlo + 1) * 64 + 896, :].rearrange(
                    "(t p) c -> p t c", p=128), (), [("Vw", wb)])
            NW = 256 if window else 0

            def win_params(qi):
                if not window:
                    return 0, 0
                r = 8 * (ci - 1) + qi
                rs = min(max(r - 4, 0), 56)
                return rs, rs - r + 7

            its = [(qi, pr) for qi in range(nrow) for pr in range(4)]
            slot = {}

            def emit_S(n):
                qi, pr = its[n]
                rs, a0 = win_params(qi)
                s_ = sc.ring("S", 2)
                slot[n] = s_
                skey = ("S", s_)
                qs = [qT[qb][half * 64:(half + 1) * 64, pr, qi * 64:(qi + 1) * 64] for half in range(2)]
                if window:
                    for p in range(4):
                        woff = (rs + 2 * p - wlo) * 64
                        for half in range(2):
                            pb0 = half * 64
                            sc.mm(ps_all[:, 2 * s_ + half, p * 64:(p + 1) * 64],
                                  kTw[wb][pb0:pb0 + 64, pr, woff:woff + 128], qs[half], True, True,
                                  [("kTw", wb), ("qT", qb)], [skey], tp=(pb0, 0))
                for t in range(2):
                    for half in range(2):
                        pb0 = half * 64
                        sc.mm(ps_all[:, 2 * s_ + half, NW + t * 64:NW + (t + 1) * 64],
                              kTc[pb0:pb0 + 64, pr, t * 128:(t + 1) * 128], qs[half], True, True,
                              ["kTc", ("qT", qb)], [skey], tp=(pb0, 0))

            emit_S(0)
            pending = []
            for n, (qi, pr) in enumerate(its):
                if n + 1 < len(its):
                    emit_S(n + 1)
                if pending and n >= pending[0][0]:
                    pending.pop(0)[1]()
                rs, a0 = win_params(qi)
                s_ = slot.pop(n)
                skey = ("S", s_)
                pbuf = sc.ring("P", 3)
                ncol = NW + 128
                sc.act(P[pbuf][:, :, 0:ncol], ps_all[:, 2 * s_:2 * s_ + 2, 0:ncol], AF.Exp, [skey],
                       [("P", pbuf, 0), ("P", pbuf, 1)], scale=scale)
                if window:
                    for half in range(2):
                        h = 2 * pr + half
                        eb = Tb[:, h, a0 * 64:a0 * 64 + 512].rearrange("p (a two c) -> p a two c", two=2, c=64)[:, :, 0, :]
                        pv = P[pbuf][:, half, 0:256].rearrange("p (a c) -> p a c", c=64)
                        sc.tt("dve", pv, pv, eb, ALU.mult, [("P", pbuf, half), "Tb"],
                              [("P", pbuf, half)])
                par = (rs - wlo) % 2
                pidx0 = (rs - wlo - par) // 2
                Vw = Vw1[wb] if par else Vw0[wb]
                obanks = (4, 5) if qi % 2 == 0 else (6, 7)
                for half in range(2):
                    h = 2 * pr + half
                    ob = obanks[h // 4]
                    o_ap = ps_all[0:64, ob, (h % 4) * 65:(h % 4 + 1) * 65]
                    first = True
                    if window:
                        for p in range(4):
                            sc.mm(o_ap, P[pbuf][:, half, p * 64:(p + 1) * 64], Vw[:, pidx0 + p, h * 65:(h + 1) * 65],
                                  first, False, [("P", pbuf, half), ("Vw", wb)], [("ps", ob)])
                            first = False
                    for t in range(2):
                        sc.mm(o_ap, P[pbuf][:, half, NW + t * 64:NW + (t + 1) * 64], Vc[:, t, h * 65:(h + 1) * 65],
                              first, t == 1, [("P", pbuf, half), "Vc"], [("ps", ob)])
                        first = False
                if pr != 3:
                    continue
                rr0 = t0 + qi * 64
                gb = sc.ring("sgr", 2)
                sc.dma(sgr[gb][:], sg_d[rr0:rr0 + 64, :], (), [("sgr", gb)])
                for half in range(2):
                    ov = ps_all[0:64, obanks[half], 0:260].rearrange("p (h d) -> p h d", d=65)
                    sc.recip(R[:, half * 4:(half + 1) * 4], ov[:, :, 64], [("ps", obanks[half])], ["R"])
                    sc.tt("dve", Yo[:, half * 256:(half + 1) * 256].rearrange("p (h d) -> p h d", d=64), ov[:, :, 0:64],
                          R[:, half * 4:(half + 1) * 4].unsqueeze(2).to_broadcast([64, 4, 64]), ALU.mult,
                          [("ps", obanks[half]), "R"], ["Yo"])
                yb_ = sc.ring("Yrow", 2)
                sc.tt("pool", Y[yb_][:], Yo[:], sgr[gb][:], ALU.mult, ["Yo", ("sgr", gb)], [("Y", yb_)])

                def row_tr(qi=qi, tbk=obanks[0], yb_=yb_):
                    for i4 in range(4):
                        sc.tr(ps_all[:, tbk, i4 * 64:(i4 + 1) * 64], Y[yb_][:, i4 * 128:(i4 + 1) * 128],
                              ident[0:64, 0:64], [("Y", yb_), "ident"], [("ps", tbk)])
                    sc.cp("dve", mxT[:, 0:4, qi * 64:(qi + 1) * 64],
                          ps_all[:, tbk, 0:256].rearrange("p (k t) -> p k t", k=4), [("ps", tbk)], ["mxT"])

                pending.append((n + 3, row_tr))
            while pending:
                pending.pop(0)[1]()
            for qt in range(nq // 128):
                r0 = t0 + qt * 128
                yb = sc.ring("yg", 2)
                sc.dma(yg[yb][:], yg_d[r0:r0 + 128, :], (), [("yg", yb)])
                for i4 in range(4):
                    sc.tr(ps_all[:, 7, i4 * 128:(i4 + 1) * 128], yg[yb][:, i4 * 128:(i4 + 1) * 128], ident[:],
                          [("yg", yb), "ident"], [("ps", 7)])
                sc.cp("act", mxT[:, 4:8, qt * 128:(qt + 1) * 128],
                      ps_all[:, 7, :].rearrange("p (k t) -> p k t", k=4), [("ps", 7)], ["mxT"])
            for qt in range(nq // 128):
                out_proj_tile(nc, sc, l, last, src_x, xs, out, t0 + qt * 128, mxT, qt * 128, Wo, modG, who,
                              xt, junk, ss2, rstd, tmp, ps_all, banks=(4, 5) if qt % 2 == 0 else (6, 7))
        sc.flush()


def host_constants():
    f32 = np.float32
    nf = 8
    inv_freq = (np.float32(10000.0) ** (-np.arange(nf, dtype=f32) / nf)).astype(f32)
    t = np.arange(S)
    row = (t // GW).astype(f32)
    col = (t % GW).astype(f32)
    ropec = np.ones((128, T), f32)
    ropes = np.zeros((128, T), f32)
    for p in range(128):
        d = p % 32
        pos = row if d < 16 else col
        dd = d % 16
        f = dd % 8
        ang = (pos * inv_freq[f]).astype(f32)
        ropec[p, L:] = np.cos(ang).astype(f32)
        sn = np.sin(ang).astype(f32)
        ropes[p, L:] = -sn if dd < 8 else sn
    identf = np.eye(128, dtype=f32)
    bf = ml_dtypes.bfloat16
    n = np.arange(S, dtype=np.int64)
    ang = (2.0 * np.pi / S) * ((n[:, None] * n[None, :]) % S).astype(np.float64)
    CN = (np.cos(ang) / 64.0).astype(f32)
    SN = (-np.sin(ang) / 64.0).astype(f32)

    def tile_dft(M):
        A = M.reshape(32, 128, 16, 256)
        return np.ascontiguousarray(A.transpose(2, 1, 0, 3).reshape(16, 128, 32 * 256)).astype(bf)

    dftc = tile_dft(CN)
    dfts = tile_dft(SN)
    n2 = np.arange(L, dtype=np.int64)
    ang2 = (2.0 * np.pi / L) * ((n2[:, None] * n2[None, :]) % L).astype(np.float64)
    CL = (np.cos(ang2) / 16.0).astype(f32).reshape(2, 128, 256).transpose(1, 0, 2).reshape(128, 512)
    SL = (-np.sin(ang2) / 16.0).astype(f32).reshape(2, 128, 256).transpose(1, 0, 2).reshape(128, 512)
    c64 = np.arange(64, dtype=np.int64)
    ang3 = (2.0 * np.pi / 64) * ((c64[:, None] * c64[None, :]) % 64).astype(np.float64)
    C64 = np.cos(ang3) / 8.0
    S64 = np.sin(ang3) / 8.0
    cs64 = np.zeros((128, 256), f32)
    for g in range(2):
        cs64[g * 64:(g + 1) * 64, g * 64:(g + 1) * 64] = C64
        cs64[g * 64:(g + 1) * 64, 128 + g * 64:128 + (g + 1) * 64] = S64
    return dict(ropec=ropec, ropes=ropes, identf=identf, dftc=dftc, dfts=dfts,
                dftc_l=np.ascontiguousarray(CL).astype(bf), dfts_l=np.ascontiguousarray(SL).astype(bf),
                cs64=cs64.astype(bf))


def host_layout(inputs):
    f32 = np.float32
    w_in_even = np.asarray(inputs["w_in_even"], f32)
    idx = np.arange(512)
    d = idx % 16
    sw = np.where(d < 8, idx + 8, idx - 8)
    q = w_in_even[:, :, 0:512]
    k = w_in_even[:, :, 512:1024]
    rest = w_in_even[:, :, 1024:]
    w_in_e = np.ascontiguousarray(np.concatenate([q, q[:, :, sw], k, k[:, :, sw], rest], axis=2))
    rpb = np.asarray(inputs["rpb_c"], f32)
    cq = np.arange(GW)
    cs = np.clip(cq - 8, 0, GW - 16)
    cp = np.arange(GW)
    inwin = (cp[:, None] >= cs[None, :]) & (cp[:, None] < cs[None, :] + 16)
    coff = np.clip(cp[:, None] - cq[None, :] + 15, 0, 30)
    g = rpb[:, :, :, coff]
    g = np.where(inwin[None, None, None], g, f32(NEG))
    rpbT = np.ascontiguousarray(g.transpose(0, 1, 3, 2, 4).reshape(2, 8, 64, 15 * 64)).astype(f32)
    common = dict(
        w_mod=np.ascontiguousarray(inputs["w_mod"], f32), b_mod=np.ascontiguousarray(inputs["b_mod"], f32),
        norm_pre=np.ascontiguousarray(inputs["norm_pre"], f32), norm_post=np.ascontiguousarray(inputs["norm_post"], f32),
        w_in_e=w_in_e, w_in_o=np.ascontiguousarray(inputs["w_in_odd"], f32),
        w_out=np.ascontiguousarray(inputs["w_out"], f32),
        lam_a=np.ascontiguousarray(np.asarray(inputs["lam_a"], f32).reshape(2, 128)),
        subln_a=np.ascontiguousarray(inputs["subln_a"], f32),
        conv_b=np.ascontiguousarray(np.asarray(inputs["conv_b"], f32).reshape(2, 3, 4, 128).transpose(0, 3, 2, 1).reshape(2, 128, 12)),
        rpbT=rpbT)
    common.update(host_constants())
    x = np.asarray(inputs["x"], f32)
    ctx = np.asarray(inputs["ctx"], f32)
    c = np.asarray(inputs["c"], f32)
    c_ctx = np.asarray(inputs["c_ctx"], f32)
    maps = []
    for b in range(8):
        m = dict(common)
        m["x_in"] = np.ascontiguousarray(np.concatenate([ctx[b], x[b]], axis=0))
        m["cvec"] = np.ascontiguousarray(np.stack([c_ctx, c[b]], axis=0).reshape(2, 8, 128).transpose(2, 0, 1).reshape(128, 16))
        maps.append(m)
    return maps


def kernel(**inputs):
    maps = host_layout(inputs)
    nc = build(DEPTH)
    res = run_bass_kernel_spmd(nc, maps, core_ids=list(range(8)))
    return np.stack([np.asarray(r["out"], np.float32) for r in res.results], axis=0)
```
